# Optimizing a Trainium2 kernel written in Bass

```python
import math
import jax, jax.numpy as jnp
from jax import lax
import numpy as np

D_MODEL = 1024
BATCH = 8
SEQ = 8192
DEPTH = 4

HEAD_DIM = 64
N_HEADS_TOTAL = D_MODEL // HEAD_DIM
N_HEADS_A = N_HEADS_TOTAL // 2
N_HEADS_B = N_HEADS_TOTAL // 4
N_HEADS_C = N_HEADS_TOTAL - N_HEADS_A - N_HEADS_B
DIFF_HALF = HEAD_DIM // 2
WIDTH_A = N_HEADS_A * HEAD_DIM
WIDTH_B = N_HEADS_B * HEAD_DIM
WIDTH_C = N_HEADS_C * HEAD_DIM
MIX_WIDTH = WIDTH_A + WIDTH_B + WIDTH_C
SPLITS = [int(c) for c in np.cumsum([WIDTH_A] * 3 + [WIDTH_B] * 3 + [WIDTH_C] * 2)]
D_FF = D_MODEL
DILATED_BRANCHES = ((128, 1), (512, 4), (2048, 16))
Q_BLOCK = 128
N_BUCKETS = 32
MAX_DISTANCE = 2048
N_BIAS_HEADS = N_HEADS_A + N_HEADS_B
EPS = 1e-6
NEG_INF = -1e30
SB_MASK = 1e4

kernel_name = 'hybrid_dilated_diff_stickbreak_macaron'


def rms_norm(x, gain):
    xf = x.astype(jnp.float32)
    y = xf * lax.rsqrt(jnp.mean(xf * xf, axis=-1, keepdims=True) + EPS)
    return (y * gain.astype(jnp.float32)).astype(x.dtype)


def swiglu(x, w_gate, w_up, w_down):
    return (jax.nn.silu(x @ w_gate) * (x @ w_up)) @ w_down


def t5_bucket(dist):
    dist = jnp.maximum(dist, 0)
    max_exact = N_BUCKETS // 2
    d_f = jnp.maximum(dist, 1).astype(jnp.float32)
    large = max_exact + (jnp.log(d_f / max_exact) / math.log(MAX_DISTANCE / max_exact)
                         * (N_BUCKETS - max_exact)).astype(jnp.int32)
    large = jnp.minimum(large, N_BUCKETS - 1)
    return jnp.where(dist < max_exact, dist, large)


def split_heads(t, n_heads, dh):
    b, s, _ = t.shape
    return t.reshape(b, s, n_heads, dh).transpose(0, 2, 1, 3)


def merge_heads(t):
    b, h, s, dh = t.shape
    return t.transpose(0, 2, 1, 3).reshape(b, s, h * dh)


def dilated_bias_masks(bias_a, seq):
    out = []
    i = jnp.arange(Q_BLOCK, dtype=jnp.int32)[:, None]
    j = jnp.arange(2 * Q_BLOCK, dtype=jnp.int32)[None, :]
    for window, dil in DILATED_BRANCHES:
        n = window // dil
        nb = -(-(seq // dil) // Q_BLOCK)
        off = i + n - j
        blk = jnp.arange(nb, dtype=jnp.int32)[:, None, None]
        valid = (off >= 0) & (off <= n) & (blk * Q_BLOCK - n + j >= 0)
        bias = jnp.moveaxis(jnp.take(bias_a, t5_bucket(off * dil), axis=0), -1, 0)
        out.append(jnp.where(valid[None], bias[:, None], NEG_INF))
    return out


def dilated_attention(q, k, v, bias_masks):
    b, h, s, dh = q.shape
    outs, lses = [], []
    for (window, dil), bm in zip(DILATED_BRANCHES, bias_masks):
        n = window // dil
        length = s // dil
        nb = bm.shape[1]
        padl = nb * Q_BLOCK

        def sub(t):
            return t.reshape(b, h, length, dil, dh).transpose(0, 1, 3, 2, 4)

        def band(t):
            tp = jnp.pad(sub(t), ((0, 0), (0, 0), (0, 0), (n, padl - length), (0, 0)))
            tp = tp.reshape(b, h, dil, nb + 1, Q_BLOCK, dh)
            return jnp.concatenate([tp[:, :, :, :-1], tp[:, :, :, 1:]], axis=4)

        qs = jnp.pad(sub(q), ((0, 0), (0, 0), (0, 0), (0, padl - length), (0, 0)))
        qs = qs.reshape(b, h, dil, nb, Q_BLOCK, dh)
        logits = jnp.einsum('bhrnqd,bhrnkd->bhrnqk', qs, band(k)) + bm[None, :, None]
        lse = jax.nn.logsumexp(logits, axis=-1)
        o = jnp.einsum('bhrnqk,bhrnkd->bhrnqd', jnp.exp(logits - lse[..., None]), band(v))
        o = o.reshape(b, h, dil, padl, dh)[:, :, :, :length].transpose(0, 1, 3, 2, 4)
        lse = lse.reshape(b, h, dil, padl)[..., :length].transpose(0, 1, 3, 2)
        outs.append(o.reshape(b, h, s, dh))
        lses.append(lse.reshape(b, h, s))
    wts = jax.nn.softmax(jnp.stack(lses), axis=0)
    return jnp.sum(wts[..., None] * jnp.stack(outs), axis=0)


def diff_attention(q, k, v, lam, dist_bias):
    s = q.shape[3]
    outs = []
    for i in range(s // Q_BLOCK):
        t0, kl = i * Q_BLOCK, (i + 1) * Q_BLOCK
        tq = t0 + jnp.arange(Q_BLOCK, dtype=jnp.int32)
        dist = tq[:, None] - jnp.arange(kl, dtype=jnp.int32)[None, :]
        bias = jnp.where(dist >= 0, jnp.take(dist_bias, jnp.maximum(dist, 0), axis=1), NEG_INF)
        logits = jnp.einsum('bhmqd,bhmkd->bhmqk', q[:, :, :, t0:kl], k[:, :, :, :kl]) + bias[:, None]
        e = jnp.exp(logits - jnp.max(logits, axis=-1, keepdims=True))
        pv = jnp.einsum('bhmqk,bhkd->bhmqd', e, v[:, :, :kl]) / jnp.sum(e, axis=-1)[..., None]
        outs.append(pv[:, :, 0] - lam * pv[:, :, 1])
    return jnp.concatenate(outs, axis=2)


def stick_breaking_attention(q, k, v):
    b, h, s, dh = q.shape
    ar = jnp.arange(Q_BLOCK, dtype=jnp.int32)
    incl = (ar[:, None] >= ar[None, :]).astype(q.dtype)
    outs = []
    for i in range(s // Q_BLOCK):
        t0, nk = i * Q_BLOCK, i + 1
        kl = nk * Q_BLOCK
        tq = t0 + ar
        strict = jnp.arange(kl, dtype=jnp.int32)[None, :] < tq[:, None]
        z = jnp.where(strict, jnp.einsum('bhqd,bhkd->bhqk', q[:, :, t0:kl], k[:, :, :kl]), -SB_MASK)
        lnot = (-jax.nn.softplus(z)).reshape(b, h, Q_BLOCK, nk, Q_BLOCK)
        within = jnp.einsum('bhqnj,js->bhqns', lnot, incl)
        later = (jnp.arange(nk)[:, None] > jnp.arange(nk)[None, :]).astype(q.dtype)
        carry = jnp.einsum('bhqn,nm->bhqm', jnp.sum(lnot, axis=-1), later)
        log_a = z + (within + carry[..., None]).reshape(b, h, Q_BLOCK, kl)
        outs.append(jnp.einsum('bhqk,bhkd->bhqd', jnp.exp(log_a), v[:, :, :kl]))
    return jnp.concatenate(outs, axis=2)


def token_mix(h, w_in, w_out, q_norm_a, k_norm_a, q_norm_b, k_norm_b,
              lambda_q1, lambda_k1, lambda_q2, lambda_k2, diff_subln, a_masks, b_dist_bias, layer):
    b, s, _ = h.shape
    proj = (h @ w_in).astype(jnp.float32)
    qa, ka, va, qb, kb, vb, qc, kc, vc = jnp.split(proj, SPLITS, axis=-1)

    qa = rms_norm(split_heads(qa, N_HEADS_A, HEAD_DIM), q_norm_a) * (HEAD_DIM ** -0.5)
    ka = rms_norm(split_heads(ka, N_HEADS_A, HEAD_DIM), k_norm_a)
    out_a = dilated_attention(qa, ka, split_heads(va, N_HEADS_A, HEAD_DIM), a_masks)

    def two_maps(t, g):
        t = split_heads(t, N_HEADS_B, HEAD_DIM).reshape(b, N_HEADS_B, s, 2, DIFF_HALF)
        return rms_norm(t, g).transpose(0, 1, 3, 2, 4)
    lam_init = 0.8 - 0.6 * math.exp(-0.3 * layer)
    lam = (jnp.exp(jnp.sum(lambda_q1.astype(jnp.float32) * lambda_k1.astype(jnp.float32)))
           - jnp.exp(jnp.sum(lambda_q2.astype(jnp.float32) * lambda_k2.astype(jnp.float32)))
           + lam_init)
    out_b = diff_attention(two_maps(qb, q_norm_b) * (DIFF_HALF ** -0.5), two_maps(kb, k_norm_b),
                           split_heads(vb, N_HEADS_B, HEAD_DIM), lam, b_dist_bias)
    out_b = rms_norm(out_b, diff_subln) * (1.0 - lam_init)

    out_c = stick_breaking_attention(split_heads(qc, N_HEADS_C, HEAD_DIM) * (HEAD_DIM ** -0.5),
                                     split_heads(kc, N_HEADS_C, HEAD_DIM),
                                     split_heads(vc, N_HEADS_C, HEAD_DIM))

    mixed = jnp.concatenate([merge_heads(out_a), merge_heads(out_b), merge_heads(out_c)], axis=-1)
    return mixed.astype(h.dtype) @ w_out


def setup_inputs(seed: int = 0) -> dict:
    key = jax.random.key(seed)
    ks = jax.random.split(key, 22)
    f32 = jnp.float32

    def nrm(k, shape, scale):
        return scale * jax.random.normal(k, shape, f32)

    def gain(k, shape):
        return 1.0 + 0.01 * jax.random.normal(k, shape, f32)

    return {
        'x': nrm(ks[0], (BATCH, SEQ, D_MODEL), 1.0),
        'rel_bias': nrm(ks[1], (N_BUCKETS, N_BIAS_HEADS), 0.5),
        'ffn1_norm': gain(ks[2], (DEPTH, D_MODEL)),
        'ffn1_w_gate': nrm(ks[3], (DEPTH, D_MODEL, D_FF), D_MODEL ** -0.5),
        'ffn1_w_up': nrm(ks[4], (DEPTH, D_MODEL, D_FF), D_MODEL ** -0.5),
        'ffn1_w_down': nrm(ks[5], (DEPTH, D_FF, D_MODEL), D_FF ** -0.5),
        'mix_norm': gain(ks[6], (DEPTH, D_MODEL)),
        'w_in': nrm(ks[7], (DEPTH, D_MODEL, 3 * MIX_WIDTH), D_MODEL ** -0.5),
        'q_norm_a': gain(ks[8], (DEPTH, HEAD_DIM)),
        'k_norm_a': gain(ks[9], (DEPTH, HEAD_DIM)),
        'q_norm_b': gain(ks[10], (DEPTH, DIFF_HALF)),
        'k_norm_b': gain(ks[11], (DEPTH, DIFF_HALF)),
        'lambda_q1': nrm(ks[12], (DEPTH, DIFF_HALF), 0.1),
        'lambda_k1': nrm(ks[13], (DEPTH, DIFF_HALF), 0.1),
        'lambda_q2': nrm(ks[14], (DEPTH, DIFF_HALF), 0.1),
        'lambda_k2': nrm(ks[15], (DEPTH, DIFF_HALF), 0.1),
        'diff_subln': gain(ks[16], (DEPTH, HEAD_DIM)),
        'w_out': nrm(ks[17], (DEPTH, MIX_WIDTH, D_MODEL), MIX_WIDTH ** -0.5),
        'ffn2_norm': gain(ks[18], (DEPTH, D_MODEL)),
        'ffn2_w_gate': nrm(ks[19], (DEPTH, D_MODEL, D_FF), D_MODEL ** -0.5),
        'ffn2_w_up': nrm(ks[20], (DEPTH, D_MODEL, D_FF), D_MODEL ** -0.5),
        'ffn2_w_down': nrm(ks[21], (DEPTH, D_FF, D_MODEL), D_FF ** -0.5),
    }


def reference(x, rel_bias, ffn1_norm, ffn1_w_gate, ffn1_w_up, ffn1_w_down, mix_norm, w_in,
              q_norm_a, k_norm_a, q_norm_b, k_norm_b, lambda_q1, lambda_k1, lambda_q2, lambda_k2,
              diff_subln, w_out, ffn2_norm, ffn2_w_gate, ffn2_w_up, ffn2_w_down):
    seq = x.shape[1]
    rb = rel_bias.astype(jnp.float32)
    a_masks = dilated_bias_masks(rb[:, :N_HEADS_A], seq)
    b_dist_bias = jnp.take(rb[:, N_HEADS_A:], t5_bucket(jnp.arange(seq, dtype=jnp.int32)), axis=0).T
    for layer in range(DEPTH):
        h = rms_norm(x, ffn1_norm[layer])
        x = x + 0.5 * swiglu(h, ffn1_w_gate[layer], ffn1_w_up[layer], ffn1_w_down[layer])
        h = rms_norm(x, mix_norm[layer])
        x = x + token_mix(h, w_in[layer], w_out[layer], q_norm_a[layer], k_norm_a[layer],
                          q_norm_b[layer], k_norm_b[layer], lambda_q1[layer], lambda_k1[layer],
                          lambda_q2[layer], lambda_k2[layer], diff_subln[layer],
                          a_masks, b_dist_bias, layer)
        h = rms_norm(x, ffn2_norm[layer])
        x = x + 0.5 * swiglu(h, ffn2_w_gate[layer], ffn2_w_up[layer], ffn2_w_down[layer])
    return x
```

```python
import contextlib
import math
import numpy as np
import concourse.bass as bass
import concourse.mybir as mybir
from concourse.bass_utils import run_bass_kernel_spmd

F32 = mybir.dt.float32
BF16 = mybir.dt.bfloat16
AF = mybir.ActivationFunctionType
ALU = mybir.AluOpType

D = 1024
HD = 64
N_BUCKETS = 32
MAX_DISTANCE = 2048
EPS = 1e-6
NEGM = -30000.0
BRANCHES = ((128, 1), (512, 4), (2048, 16))
N_DMA_SLOTS = 8

QA0, KA0, VA0, QB0, KB0, VB0, QC0, KC0, VC0 = 0, 512, 1024, 1536, 1792, 2048, 2304, 2560, 2816


def t5_bucket_np(dist):
    dist = np.maximum(dist, 0)
    max_exact = N_BUCKETS // 2
    d_f = np.maximum(dist, 1).astype(np.float32)
    large = max_exact + (np.log(d_f / np.float32(max_exact)) / np.float32(math.log(MAX_DISTANCE / max_exact))
                         * np.float32(N_BUCKETS - max_exact)).astype(np.int32)
    large = np.minimum(large, N_BUCKETS - 1)
    return np.where(dist < max_exact, dist, large)


def _far_d():
    d = 0
    while True:
        if np.all(t5_bucket_np(np.arange(max(d - 127, 0), d + 1024)) == N_BUCKETS - 1) and d - 127 >= 0:
            return d
        d += 128


FAR_D = _far_d()
WT_B = FAR_D + 384 + 512 - 128
UB = WT_B + 127
UA = 256 + 127
UTOT = UB + 3 * UA


def make_consts():
    c = {}
    c["ident"] = np.eye(128, dtype=np.float32)
    c["jflip"] = np.eye(128, dtype=np.float32)[::-1].copy()
    j = np.arange(128)[:, None]
    s = np.arange(128)[None, :]
    c["trineg"] = -(j >= s).astype(np.float32)
    c["onesneg"] = -np.ones((128, 128), np.float32)
    bd64 = np.zeros((128, 128), np.float32)
    for g in range(2):
        bd64[g * 64:(g + 1) * 64, g * 64:(g + 1) * 64] = 1.0 / 64
    bd32 = np.zeros((128, 128), np.float32)
    for g in range(4):
        bd32[g * 32:(g + 1) * 32, g * 32:(g + 1) * 32] = 1.0 / 32
    c["bd64"] = bd64
    c["bd32"] = bd32
    mc = np.zeros((128, 4, 4, 128), np.float32)
    for cc in range(4):
        for qs in range(4):
            if qs < cc:
                mc[:, cc, qs, :] = -1e4
            elif qs == cc:
                mc[:, cc, qs, :] = np.where(j < s, 0.0, -1e4)
    c["maskc"] = mc.reshape(128, 4 * 512)
    consts = np.concatenate([c[k] for k in ["ident", "jflip", "trineg", "onesneg", "bd64", "bd32", "maskc"]], axis=1)
    oh = np.zeros((33, UTOT), np.float32)
    u = np.arange(UB)
    dist = u - 511
    b = t5_bucket_np(dist)
    oh[b[dist >= 0], u[dist >= 0]] = 1.0
    oh[32, u[dist < 0]] = 1.0
    for bi, (w, dil) in enumerate(BRANCHES):
        u = np.arange(UA)
        off = u - 127
        ok = (off >= 0) & (off <= 128)
        b = t5_bucket_np(off * dil)
        base = UB + bi * UA
        oh[b[ok], base + u[ok]] = 1.0
        oh[32, base + u[~ok]] = 1.0
    return np.ascontiguousarray(consts), oh


C_IDENT, C_JFLIP, C_TRI, C_ONES, C_BD64, C_BD32, C_MASKC = 0, 128, 256, 384, 512, 640, 768
C_TOT = 768 + 2048


class Buf:
    __slots__ = ("name", "last_w", "readers", "t")

    def __init__(self, name, t=None):
        self.name = name
        self.last_w = None
        self.readers = []
        self.t = t

    def __getitem__(self, idx):
        return self.t[idx]


class Sched:
    def __init__(self, nc, stack):
        self.nc = nc
        self.eng = {"pe": nc.tensor, "act": nc.scalar, "dve": nc.vector, "pool": nc.gpsimd, "sp": nc.sync}
        self.sem = {}
        self.cnt = {}
        for e in ("pe", "act", "dve", "pool"):
            self.sem[e] = stack.enter_context(nc.semaphore("s_" + e))
            self.cnt[e] = 0
        self.dq = {}
        for q in ("sp", "act", "pool"):
            for s in range(N_DMA_SLOTS):
                k = ("dma", q, s)
                self.sem[k] = stack.enter_context(nc.semaphore("d_%s%d" % (q, s)))
                self.cnt[k] = 0
            self.dq[q] = 0
        self.waited = {}
        self.nops = 0

    def sbuf(self, stack, name, shape, dtype):
        self.uid = getattr(self, "uid", 0) + 1
        name = "%s_%d" % (name, self.uid)
        return Buf(name, stack.enter_context(self.nc.sbuf_tensor(name, list(shape), dtype)))

    def psum(self, stack, name, shape, dtype):
        self.uid = getattr(self, "uid", 0) + 1
        name = "%s_%d" % (name, self.uid)
        return Buf(name, stack.enter_context(self.nc.psum_tensor(name, list(shape), dtype)))

    def _wait(self, e, ticket):
        if ticket is None:
            return
        k, v = ticket
        key = (e, k)
        if self.waited.get(key, 0) >= v:
            return
        self.waited[key] = v
        self.eng[e].wait_ge(self.sem[k], v)

    def _deps(self, e, reads, writes):
        for b in reads:
            self._wait(e, b.last_w)
        for b in writes:
            self._wait(e, b.last_w)
            for r in b.readers:
                self._wait(e, r)

    def _commit(self, ticket, reads, writes):
        for b in reads:
            b.readers.append(ticket)
        for b in writes:
            b.last_w = ticket
            b.readers = []

    def op(self, e, fn, reads=(), writes=()):
        self._deps(e, reads, writes)
        ins = fn(self.eng[e])
        self.cnt[e] += 1
        ins.then_inc(self.sem[e], 1)
        self._commit((e, self.cnt[e]), reads, writes)
        self.nops += 1
        return ins

    def pe(self, fn, reads=(), writes=()):
        return self.op("pe", fn, reads, writes)

    def act(self, fn, reads=(), writes=()):
        return self.op("act", fn, reads, writes)

    def dve(self, fn, reads=(), writes=()):
        return self.op("dve", fn, reads, writes)

    def pool(self, fn, reads=(), writes=()):
        return self.op("pool", fn, reads, writes)

    def pe_group(self, fns, reads=(), writes=()):
        self._deps("pe", reads, writes)
        ins = None
        for fn in fns:
            ins = fn(self.eng["pe"])
        self.cnt["pe"] += 1
        ins.then_inc(self.sem["pe"], 1)
        self._commit(("pe", self.cnt["pe"]), reads, writes)
        self.nops += len(fns)
        return ins

    def dma(self, q, out, in_, reads=(), writes=(), **kw):
        s = self.dq[q] % N_DMA_SLOTS
        self.dq[q] += 1
        k = ("dma", q, s)
        if self.cnt[k] > 0:
            self._wait(q, (k, self.cnt[k]))
        self._deps(q, reads, writes)
        ins = self.eng[q].dma_start(out=out, in_=in_, **kw)
        self.cnt[k] += 16
        ins.then_inc(self.sem[k], 16)
        self._commit((k, self.cnt[k]), reads, writes)
        self.nops += 1
        return ins

    def barrier(self):
        tickets = [(k, v) for k, v in self.cnt.items() if v > 0]
        for e in ("pe", "act", "dve", "pool", "sp"):
            for t in tickets:
                self._wait(e, t)


class Rot:
    def __init__(self, items):
        self.items = items
        self.i = 0

    def next(self):
        b = self.items[self.i % len(self.items)]
        self.i += 1
        return b


class Prog:
    def __init__(self, S_len, depth, dbg=False, phases=None):
        self.SL = S_len
        self.depth = depth
        self.NT = S_len // 512
        self.NB = S_len // 128
        self.dbg = dbg
        self.phases = phases
        nc = bass.Bass("TRN2", target_bir_lowering=False)
        self.nc = nc
        ein = "ExternalInput"
        dt = nc.dram_tensor
        self.x = dt("x", [S_len, D], F32, kind=ein)
        self.rel_bias = dt("rel_bias", [32, 12], F32, kind=ein)
        L = depth
        self.w = {}
        for n, shp in [("ffn1_norm", [L, D]), ("ffn1_w_gate", [L, D, D]), ("ffn1_w_up", [L, D, D]),
                       ("ffn1_w_down", [L, D, D]), ("mix_norm", [L, D]), ("w_in", [L, D, 3 * D]),
                       ("q_norm_a", [L, 64]), ("k_norm_a", [L, 64]), ("q_norm_b", [L, 32]), ("k_norm_b", [L, 32]),
                       ("lambda_q1", [L, 32]), ("lambda_k1", [L, 32]), ("lambda_q2", [L, 32]), ("lambda_k2", [L, 32]),
                       ("diff_subln", [L, 64]), ("w_out", [L, D, D]), ("ffn2_norm", [L, D]),
                       ("ffn2_w_gate", [L, D, D]), ("ffn2_w_up", [L, D, D]), ("ffn2_w_down", [L, D, D])]:
            self.w[n] = dt(n, shp, F32, kind=ein)
        self.consts_d = dt("consts", [128, C_TOT], F32, kind=ein)
        self.onehot_d = dt("onehot", [33, UTOT], F32, kind=ein)
        self.y = dt("y", [S_len, D], F32, kind="ExternalOutput")
        sk = "ExternalOutput" if dbg else "Internal"
        self.xs = dt("xs_s", [S_len, D], F32, kind=sk)
        self.qk = dt("qk_s", [2048, S_len], BF16, kind=sk)
        self.vx = dt("vx_s", [S_len, 16 * 128], BF16, kind=sk)
        self.mixT = dt("mix_s", [D, S_len], BF16, kind=sk)
        self.tabs = dt("tabs_s", [12, UTOT], F32, kind=sk)
        self.tb_d = dt("tb_s", [4, 128, WT_B], F32, kind=sk)
        self.ta_d = dt("ta_s", [8, 128, 3 * 256], F32, kind=sk)

    def build(self):
        nc = self.nc
        with contextlib.ExitStack() as top:
            S = Sched(nc, top)
            self.S = S
            self.cb = S.sbuf(top, "cb", [128, C_TOT], BF16)
            self.cf = S.sbuf(top, "cf", [128, 256], F32)
            self.mhalf = S.sbuf(top, "mhalf", [128, 8], F32)
            self.setup_consts()
            ph = self.phases
            for l in range(self.depth):
                if ph is None or "p1" in ph or ("%d:p1" % l) in ph:
                    self.phase_p1(l)
                    S.barrier()
                if ph is None or "att" in ph or ("%d:att" % l) in ph:
                    self.phase_att(l)
                    S.barrier()
                if ph is None or "p3" in ph or ("%d:p3" % l) in ph:
                    self.phase_p3(l)
                    S.barrier()
            S.barrier()
        return nc

    def setup_consts(self):
        S = self.S
        with contextlib.ExitStack() as st:
            stg = S.sbuf(st, "c_stg", [128, C_TOT], F32)
            S.dma("sp", stg[:], self.consts_d.ap(), writes=[stg])
            S.dve(lambda e: e.tensor_copy(out=self.cb[:], in_=stg[:]), reads=[stg], writes=[self.cb])
            S.dve(lambda e: e.tensor_copy(out=self.cf[:], in_=stg[:, 0:256]), reads=[stg], writes=[self.cf])
            S.pool(lambda e: e.memset(self.mhalf[:], -0.5), writes=[self.mhalf])
            rbx = S.sbuf(st, "rbx", [33, 12], F32)
            S.dve(lambda e: e.memset(rbx[:], NEGM), writes=[rbx])
            S.dma("sp", rbx[0:32, :], self.rel_bias.ap(), writes=[rbx])
            oh = S.sbuf(st, "oh", [33, UTOT], F32)
            S.dma("sp", oh[:], self.onehot_d.ap(), writes=[oh])
            tsb = S.sbuf(st, "tsb", [12, UTOT], F32)
            pst = S.psum(st, "pst", [128, 512], F32)
            c0 = 0
            while c0 < UTOT:
                w = min(512, UTOT - c0)
                S.pe(lambda e, c0=c0, w=w: e.matmul(pst[0:12, 0:w], lhsT=rbx[:, :], rhs=oh[:, c0:c0 + w],
                                                   start=True, stop=True), reads=[rbx, oh], writes=[pst])
                S.dve(lambda e, c0=c0, w=w: e.tensor_copy(out=tsb[:, c0:c0 + w], in_=pst[0:12, 0:w]),
                      reads=[pst], writes=[tsb])
                c0 += w
            cfc = S.sbuf(st, "cfc", [12, 1], F32)
            S.dve(lambda e: e.tensor_copy(out=cfc[:], in_=tsb[:, UB - 1:UB]), reads=[tsb], writes=[cfc])
            S.dve(lambda e: e.tensor_scalar(out=tsb[:, 0:UB], in0=tsb[:, 0:UB], scalar1=cfc[:, 0:1], scalar2=None,
                                            op0=ALU.subtract), reads=[tsb, cfc], writes=[tsb])
            S.dma("sp", self.tabs.ap(), tsb[:], reads=[tsb])
            S.barrier()
            gp = S.sbuf(st, "gp", [128, WT_B], F32)
            tt = S.sbuf(st, "tt", [128, WT_B], F32)

            def expand(row, u0, width, dst_ap):
                src = bass.AP(tensor=self.tabs, offset=row * UTOT + u0, ap=[[1, 128], [1, width]])
                S.dma("sp", gp[:, 0:width], src, writes=[gp])
                c = 0
                while c < width:
                    w = min(512, width - c)
                    S.pe(lambda e, c=c, w=w: e.matmul(pst[:, 0:w], lhsT=self.cf[:, 128:256], rhs=gp[:, c:c + w],
                                                     start=True, stop=True), reads=[gp, self.cf], writes=[pst])
                    S.dve(lambda e, c=c, w=w: e.tensor_copy(out=tt[:, c:c + w], in_=pst[:, 0:w]),
                          reads=[pst], writes=[tt])
                    c += w
                S.dma("sp", dst_ap, tt[:, 0:width], reads=[tt])

            for h in range(4):
                expand(8 + h, 0, WT_B, self.tb_d.ap()[h])
            for h in range(8):
                for bi in range(3):
                    expand(h, UB + bi * UA, 256, self.ta_d.ap()[h][:, bi * 256:(bi + 1) * 256])
            S.barrier()

    def load_weight(self, st_pool, wap, dst, ncols, gcol=None, cscale=None):
        S = self.S
        for k in range(8):
            for c0 in range(0, ncols, 1024):
                stg = st_pool.next()
                S.dma("sp", stg[:], wap[k * 128:(k + 1) * 128, c0:c0 + 1024], writes=[stg])
                use_act = (self._wl % 2 == 1)
                self._wl += 1
                if use_act:
                    sc = gcol[:, k:k + 1] if gcol is not None else float(cscale or 1.0)
                    S.act(lambda e, k=k, c0=c0, stg=stg, sc=sc: e.activation(
                        out=dst[:, k, c0:c0 + 1024], in_=stg[:], func=AF.Copy, scale=sc),
                        reads=[stg] + ([gcol] if gcol is not None else []), writes=[dst])
                elif gcol is not None:
                    S.dve(lambda e, k=k, c0=c0, stg=stg: e.tensor_scalar(
                        out=dst[:, k, c0:c0 + 1024], in0=stg[:], scalar1=gcol[:, k:k + 1], scalar2=None,
                        op0=ALU.mult), reads=[stg, gcol], writes=[dst])
                else:
                    S.dve(lambda e, k=k, c0=c0, stg=stg: e.tensor_scalar(
                        out=dst[:, k, c0:c0 + 1024], in0=stg[:], scalar1=float(cscale or 1.0), scalar2=None,
                        op0=ALU.mult), reads=[stg], writes=[dst])

    def load_gcol(self, dst, vec_ap):
        src = bass.AP(tensor=vec_ap.tensor, offset=vec_ap.offset, ap=[[1, 128], [128, 8]])
        self.S.dma("sp", dst[:], src, writes=[dst], allow_slow_non_contiguous=True)

    def alloc_ffn(self, st):
        S = self.S
        f = {}
        f["sqj"] = S.sbuf(st, "sqj", [128, 1024], BF16)
        f["ss"] = S.sbuf(st, "ss", [128, 4], F32)
        f["ms"] = S.sbuf(st, "ms", [128, 4], F32)
        f["rstd"] = S.sbuf(st, "rstd", [128, 4], F32)
        f["hbf"] = [S.sbuf(st, "hbf%d" % j, [128, 1024], BF16) for j in range(4)]
        f["hT"] = S.sbuf(st, "hT", [128, 8, 512], BF16)
        f["hTp"] = [Buf("hT%d" % i) for i in range(4)]
        f["sg"] = Rot([S.sbuf(st, "sg%d" % i, [128, 512], F32) for i in range(2)])
        f["aT"] = S.sbuf(st, "aT", [128, 8, 512], BF16)
        f["aTp"] = [Buf("aT%d" % i) for i in range(8)]
        return f

    def norm_transpose(self, f, xt, xtp, tps):
        S = self.S
        ss, ms, rstd = f["ss"], f["ms"], f["rstd"]
        for j in range(4):
            S.act(lambda e, j=j: e.activation(out=f["sqj"][:], in_=xt[:, j, :], func=AF.Square,
                                              accum_out=ss[:, j:j + 1]), reads=[xtp[j]], writes=[f["sqj"], ss])
        S.dve(lambda e: e.tensor_scalar(out=ms[:], in0=ss[:], scalar1=1.0 / D, scalar2=EPS, op0=ALU.mult,
                                        op1=ALU.add), reads=[ss], writes=[ms])
        S.pool(lambda e: e.tensor_tensor(out=rstd[:], in0=ms[:], in1=self.mhalf[:, 0:4], op=ALU.pow),
               reads=[ms, self.mhalf], writes=[rstd])
        for j in range(4):
            S.dve(lambda e, j=j: e.tensor_scalar(out=f["hbf"][j][:], in0=xt[:, j, :], scalar1=rstd[:, j:j + 1],
                                                 scalar2=None, op0=ALU.mult),
                  reads=[xtp[j], rstd], writes=[f["hbf"][j]])
        for kk in range(4):
            tp = tps.next()
            fns = []
            for k2 in range(2):
                k = kk * 2 + k2
                for j in range(4):
                    fns.append(lambda e, k=k, k2=k2, j=j, tp=tp: e.transpose(
                        out=tp[:, k2 * 512 + j * 128:k2 * 512 + (j + 1) * 128],
                        in_=f["hbf"][j][:, k * 128:(k + 1) * 128], identity=self.cb[:, C_IDENT:C_IDENT + 128]))
            S.pe_group(fns, reads=f["hbf"] + [self.cb], writes=[tp])
            eng = "act" if kk % 2 == 0 else "dve"
            if eng == "act":
                S.act(lambda e, kk=kk, tp=tp: e.activation(out=f["hT"][:, 2 * kk:2 * kk + 2, :],
                                                           in_=tp[:].rearrange("p (a b) -> p a b", a=2),
                                                           func=AF.Copy), reads=[tp], writes=[f["hTp"][kk]])
            else:
                S.dve(lambda e, kk=kk, tp=tp: e.tensor_copy(out=f["hT"][:, 2 * kk:2 * kk + 2, :],
                                                            in_=tp[:].rearrange("p (a b) -> p a b", a=2)),
                      reads=[tp], writes=[f["hTp"][kk]])

    def ffn_tile(self, f, xt, xtp, wg, wu, wd, mms, tps):
        S = self.S
        self.norm_transpose(f, xt, xtp, tps)
        hT, aT = f["hT"], f["aT"]
        for n in range(8):
            pg = mms.next()
            pu = mms.next()
            S.pe_group([lambda e, k=k, n=n, pg=pg: e.matmul(pg[:], lhsT=wg[:, k, n * 128:(n + 1) * 128],
                                                            rhs=hT[:, k, :], start=(k == 0), stop=(k == 7))
                        for k in range(8)], reads=f["hTp"] + [wg], writes=[pg])
            S.pe_group([lambda e, k=k, n=n, pu=pu: e.matmul(pu[:], lhsT=wu[:, k, n * 128:(n + 1) * 128],
                                                            rhs=hT[:, k, :], start=(k == 0), stop=(k == 7))
                        for k in range(8)], reads=f["hTp"] + [wu], writes=[pu])
            sg = f["sg"].next()
            S.act(lambda e, pg=pg, sg=sg: e.activation(out=sg[:], in_=pg[:], func=AF.Silu), reads=[pg], writes=[sg])
            S.dve(lambda e, n=n, sg=sg, pu=pu: e.tensor_tensor(out=aT[:, n, :], in0=sg[:], in1=pu[:], op=ALU.mult),
                  reads=[sg, pu], writes=[f["aTp"][n]])
        for j in range(4):
            for hf in range(2):
                py = mms.next()
                S.pe_group([lambda e, k=k, j=j, hf=hf, py=py: e.matmul(
                    py[:], lhsT=aT[:, k, j * 128:(j + 1) * 128], rhs=wd[:, k, hf * 512:(hf + 1) * 512],
                    start=(k == 0), stop=(k == 7)) for k in range(8)], reads=f["aTp"] + [wd], writes=[py])
                S.dve(lambda e, j=j, hf=hf, py=py: e.tensor_tensor(
                    out=xt[:, j, hf * 512:(hf + 1) * 512], in0=py[:], in1=xt[:, j, hf * 512:(hf + 1) * 512],
                    op=ALU.add), reads=[py, xtp[j]], writes=[xtp[j]])

    def phase_p1(self, l):
        S = self.S
        W = self.w
        xin = self.x if l == 0 else self.xs
        with contextlib.ExitStack() as st:
            self._wl = 0
            stg = Rot([S.sbuf(st, "wstg%d" % i, [128, 1024], F32) for i in range(2)])
            wg = S.sbuf(st, "wg", [128, 8, 1024], BF16)
            wu = S.sbuf(st, "wu", [128, 8, 1024], BF16)
            wd = S.sbuf(st, "wd", [128, 8, 1024], BF16)
            wi = S.sbuf(st, "wi", [128, 8, 3072], BF16)
            g1 = S.sbuf(st, "g1", [128, 8], F32)
            g2 = S.sbuf(st, "g2", [128, 8], F32)
            self.load_gcol(g1, W["ffn1_norm"].ap()[l])
            self.load_gcol(g2, W["mix_norm"].ap()[l])
            self.load_weight(stg, W["ffn1_w_gate"].ap()[l], wg, 1024, gcol=g1)
            self.load_weight(stg, W["ffn1_w_up"].ap()[l], wu, 1024, gcol=g1)
            self.load_weight(stg, W["ffn1_w_down"].ap()[l], wd, 1024, cscale=0.5)
            self.load_weight(stg, W["w_in"].ap()[l], wi, 3072, gcol=g2)
            gq = S.sbuf(st, "gq", [128, 4], F32)
            for c, (nm, width) in enumerate([("q_norm_a", 64), ("k_norm_a", 64), ("q_norm_b", 32), ("k_norm_b", 32)]):
                for r in range(128 // width):
                    src = bass.AP(tensor=W[nm], offset=l * width, ap=[[1, width], [1, 1]])
                    S.dma("sp", gq[r * width:(r + 1) * width, c:c + 1], src, writes=[gq])
            S.dve(lambda e: e.tensor_scalar(out=gq[:, 0:1], in0=gq[:, 0:1], scalar1=HD ** -0.5, scalar2=None,
                                            op0=ALU.mult), reads=[gq], writes=[gq])
            S.dve(lambda e: e.tensor_scalar(out=gq[:, 2:3], in0=gq[:, 2:3], scalar1=32 ** -0.5, scalar2=None,
                                            op0=ALU.mult), reads=[gq], writes=[gq])
            f = self.alloc_ffn(st)
            xts = [S.sbuf(st, "xt%d" % i, [128, 4, 1024], F32) for i in range(2)]
            xtps = [[Buf("xt%d_%d" % (i, j)) for j in range(4)] for i in range(2)]
            mms = Rot([S.psum(st, "mm%d" % i, [128, 512], F32) for i in range(6)])
            tps = Rot([S.psum(st, "tp%d" % i, [128, 1024], BF16) for i in range(2)])
            qkt = S.sbuf(st, "qkt", [128, 8, 512], BF16)
            vxt = S.sbuf(st, "vxt", [128, 2, 16 * 128], BF16)
            qktp = [Buf("qkt%d" % i) for i in range(8)]
            vxtp = [Buf("vxt%d" % i) for i in range(2)]
            sqb = Rot([S.sbuf(st, "sqb%d" % i, [128, 512], BF16) for i in range(2)])
            srt = Rot([S.sbuf(st, "srt%d" % i, [128, 512], F32) for i in range(2)])
            S.pool(lambda e: e.memset(vxt[:], 1.0), writes=vxtp)
            chunks = ([(QA0 + i * 128, 0) for i in range(4)] + [(KA0 + i * 128, 1) for i in range(4)] +
                      [(QB0 + i * 128, 2) for i in range(2)] + [(KB0 + i * 128, 3) for i in range(2)] +
                      [(QC0 + i * 128, 4) for i in range(2)] + [(KC0 + i * 128, 5) for i in range(2)])
            for ti in range(self.NT):
                xt, xtp = xts[ti % 2], xtps[ti % 2]
                t0 = ti * 512
                S.dma("sp", xt[:], xin.ap()[t0:t0 + 512, :].rearrange("(j p) d -> p j d", p=128), writes=xtp)
                self.ffn_tile(f, xt, xtp, wg, wu, wd, mms, tps)
                S.dma("sp", self.xs.ap()[t0:t0 + 512, :].rearrange("(j p) d -> p j d", p=128), xt[:], reads=xtp)
                self.norm_transpose(f, xt, xtp, tps)
                hT = f["hT"]
                for c, (col0, kind) in enumerate(chunks):
                    pq = mms.next()
                    S.pe_group([lambda e, k=k, col0=col0, pq=pq: e.matmul(
                        pq[:], lhsT=wi[:, k, col0:col0 + 128], rhs=hT[:, k, :], start=(k == 0), stop=(k == 7))
                        for k in range(8)], reads=f["hTp"] + [wi], writes=[pq])
                    if kind <= 3:
                        sq = sqb.next()
                        S.act(lambda e, pq=pq, sq=sq: e.activation(out=sq[:], in_=pq[:], func=AF.Square),
                              reads=[pq], writes=[sq])
                        pm = mms.next()
                        bdc = C_BD64 if kind <= 1 else C_BD32
                        S.pe(lambda e, pm=pm, sq=sq, bdc=bdc: e.matmul(pm[:], lhsT=self.cb[:, bdc:bdc + 128], rhs=sq[:],
                                                                      start=True, stop=True),
                             reads=[sq, self.cb], writes=[pm])
                        sr = srt.next()
                        S.act(lambda e, pm=pm, sr=sr: e.activation(out=sr[:], in_=pm[:], func=AF.Sqrt, bias=EPS),
                              reads=[pm], writes=[sr])
                        rr = sr
                        S.dve(lambda e, sr=sr: e.reciprocal(out=sr[:], in_=sr[:]), reads=[sr], writes=[sr])
                        S.dve(lambda e, c=c, pq=pq, rr=rr, kind=kind: e.scalar_tensor_tensor(
                            out=qkt[:, c % 8, :], in0=pq[:], scalar=gq[:, kind:kind + 1], in1=rr[:], op0=ALU.mult,
                            op1=ALU.mult), reads=[pq, rr, gq], writes=[qktp[c % 8]])
                    elif kind == 4:
                        S.act(lambda e, c=c, pq=pq: e.activation(out=qkt[:, c % 8, :], in_=pq[:], func=AF.Copy,
                                                                 scale=HD ** -0.5), reads=[pq], writes=[qktp[c % 8]])
                    else:
                        S.act(lambda e, c=c, pq=pq: e.activation(out=qkt[:, c % 8, :], in_=pq[:], func=AF.Copy),
                              reads=[pq], writes=[qktp[c % 8]])
                    if c % 8 == 7:
                        r0 = (c // 8) * 1024
                        S.dma("sp", self.qk.ap()[r0:r0 + 1024, t0:t0 + 512].rearrange("(c p) t -> p c t", p=128),
                              qkt[:], reads=qktp)
                for j in range(4):
                    pv = mms.next()
                    S.pe_group([lambda e, k=k, j=j, pv=pv: e.matmul(
                        pv[:], lhsT=hT[:, k, j * 128:(j + 1) * 128], rhs=wi[:, k, VA0:VA0 + 512],
                        start=(k == 0), stop=(k == 7)) for k in range(8)], reads=f["hTp"] + [wi], writes=[pv])
                    S.act(lambda e, j=j, pv=pv: e.activation(
                        out=vxt[:, j % 2, 0:1024].rearrange("p (h e) -> p h e", e=128)[:, :, 0:64],
                        in_=pv[:].rearrange("p (h e) -> p h e", e=64), func=AF.Copy), reads=[pv], writes=[vxtp[j % 2]])
                    pv2 = mms.next()
                    fns = []
                    for g, vc0 in enumerate((VB0, VC0)):
                        for k in range(8):
                            fns.append(lambda e, k=k, j=j, g=g, vc0=vc0, pv2=pv2: e.matmul(
                                pv2[:, g * 256:(g + 1) * 256], lhsT=hT[:, k, j * 128:(j + 1) * 128],
                                rhs=wi[:, k, vc0:vc0 + 256], start=(k == 0), stop=(k == 7)))
                    S.pe_group(fns, reads=f["hTp"] + [wi], writes=[pv2])
                    S.dve(lambda e, j=j, pv2=pv2: e.tensor_copy(
                        out=vxt[:, j % 2, 1024:2048].rearrange("p (h e) -> p h e", e=128)[:, :, 0:64],
                        in_=pv2[:].rearrange("p (h e) -> p h e", e=64)), reads=[pv2], writes=[vxtp[j % 2]])
                    if j % 2 == 1:
                        tt0 = t0 + (j - 1) * 128
                        S.dma("sp", self.vx.ap()[tt0:tt0 + 256, :].rearrange("(j p) f -> p j f", p=128), vxt[:],
                              reads=vxtp)

    def phase_p3(self, l):
        S = self.S
        W = self.w
        with contextlib.ExitStack() as st:
            self._wl = 0
            stg = Rot([S.sbuf(st, "wstg%d" % i, [128, 1024], F32) for i in range(2)])
            wg = S.sbuf(st, "wg", [128, 8, 1024], BF16)
            wu = S.sbuf(st, "wu", [128, 8, 1024], BF16)
            wd = S.sbuf(st, "wd", [128, 8, 1024], BF16)
            wo = S.sbuf(st, "wo", [128, 8, 1024], BF16)
            g1 = S.sbuf(st, "g1", [128, 8], F32)
            self.load_gcol(g1, W["ffn2_norm"].ap()[l])
            self.load_weight(stg, W["w_out"].ap()[l], wo, 1024)
            self.load_weight(stg, W["ffn2_w_gate"].ap()[l], wg, 1024, gcol=g1)
            self.load_weight(stg, W["ffn2_w_up"].ap()[l], wu, 1024, gcol=g1)
            self.load_weight(stg, W["ffn2_w_down"].ap()[l], wd, 1024, cscale=0.5)
            f = self.alloc_ffn(st)
            xts = [S.sbuf(st, "xt%d" % i, [128, 4, 1024], F32) for i in range(2)]
            xtps = [[Buf("xt%d_%d" % (i, j)) for j in range(4)] for i in range(2)]
            mxs = [S.sbuf(st, "mx%d" % i, [128, 8, 512], BF16) for i in range(2)]
            mms = Rot([S.psum(st, "mm%d" % i, [128, 512], F32) for i in range(6)])
            tps = Rot([S.psum(st, "tp%d" % i, [128, 1024], BF16) for i in range(2)])
            for ti in range(self.NT):
                xt, xtp, mx = xts[ti % 2], xtps[ti % 2], mxs[ti % 2]
                t0 = ti * 512
                S.dma("sp", xt[:], self.xs.ap()[t0:t0 + 512, :].rearrange("(j p) d -> p j d", p=128), writes=xtp)
                S.dma("sp", mx[:], self.mixT.ap()[:, t0:t0 + 512].rearrange("(c p) t -> p c t", p=128), writes=[mx])
                for j in range(4):
                    for hf in range(2):
                        py = mms.next()
                        S.pe_group([lambda e, k=k, j=j, hf=hf, py=py: e.matmul(
                            py[:], lhsT=mx[:, k, j * 128:(j + 1) * 128], rhs=wo[:, k, hf * 512:(hf + 1) * 512],
                            start=(k == 0), stop=(k == 7)) for k in range(8)], reads=[mx, wo], writes=[py])
                        S.dve(lambda e, j=j, hf=hf, py=py: e.tensor_tensor(
                            out=xt[:, j, hf * 512:(hf + 1) * 512], in0=py[:], in1=xt[:, j, hf * 512:(hf + 1) * 512],
                            op=ALU.add), reads=[py, xtp[j]], writes=[xtp[j]])
                self.ffn_tile(f, xt, xtp, wg, wu, wd, mms, tps)
                dst = self.y if l == self.depth - 1 else self.xs
                S.dma("sp", dst.ap()[t0:t0 + 512, :].rearrange("(j p) d -> p j d", p=128), xt[:], reads=xtp)

    def load_head(self, qt_, kt_, vx_, qrow, krow, vhead):
        S = self.S
        S.dma("sp", qt_[0:64, :], self.qk.ap()[qrow:qrow + 64, :], writes=[qt_])
        S.dma("sp", kt_[0:64, :], self.qk.ap()[krow:krow + 64, :], writes=[kt_])
        if vx_ is not None:
            src = self.vx.ap()[:, vhead * 128:(vhead + 1) * 128].rearrange("(n p) e -> p n e", p=128)
            S.dma("sp", vx_[:], src, writes=[vx_])

    def phase_att(self, l):
        import os
        sel = os.environ.get("ATT_SEL", "cba")
        if "c" in sel:
            self.att_c(l)
            self.S.barrier()
        if "b" in sel:
            self.att_b(l)
            self.S.barrier()
        if "a" in sel:
            self.att_a(l)

    def att_c(self, l):
        S = self.S
        SL, NT, NB = self.SL, self.NT, self.NB
        cb = self.cb
        with contextlib.ExitStack() as st:
            qts = [S.sbuf(st, "cq%d" % i, [128, SL], BF16) for i in range(2)]
            kts = [S.sbuf(st, "ck%d" % i, [128, SL], BF16) for i in range(2)]
            for t_ in qts + kts:
                S.pool(lambda e, t_=t_: e.memset(t_[64:128, :], 0.0), writes=[t_])
            vxs = [S.sbuf(st, "cv%d" % i, [128, NB, 128], BF16) for i in range(2)]
            ets = [S.sbuf(st, "ce%d" % i, [128, 512], F32) for i in range(2)]
            lbs = [S.sbuf(st, "cl%d" % i, [128, 512], BF16) for i in range(3)]
            abs_ = [S.sbuf(st, "ca%d" % i, [128, 512], BF16) for i in range(2)]
            lsf = S.sbuf(st, "clsf", [128, 512], F32)
            lsb = [S.sbuf(st, "clsb%d" % i, [128, 512], BF16) for i in range(4)]
            osb = Rot([S.sbuf(st, "cos%d" % i, [64, 512], BF16) for i in range(2)])
            z1s = [S.psum(st, "cz1_%d" % i, [128, 512], F32) for i in range(2)]
            z2s = [S.psum(st, "cz2_%d" % i, [128, 512], F32) for i in range(2)]
            ots = [S.psum(st, "cot%d" % i, [128, 512], F32) for i in range(2)]
            ident = cb[:, C_IDENT:C_IDENT + 128]
            for h in range(4):
                if h == 0:
                    self.load_head(qts[0], kts[0], vxs[0], 1536 + 0 * 64, 1792 + 0 * 64, 12)
                if h + 1 < 4:
                    self.load_head(qts[(h + 1) % 2], kts[(h + 1) % 2], vxs[(h + 1) % 2],
                                   1536 + (h + 1) * 64, 1792 + (h + 1) * 64, 12 + h + 1)
                qT, kT, vx = qts[h % 2], kts[h % 2], vxs[h % 2]
                steps = []
                for qt in range(NT):
                    kbs = list(range(4 * qt + 3, -1, -1))
                    for idx, kb in enumerate(kbs):
                        steps.append((qt, idx, kb, len(kbs)))
                n = len(steps)

                def zmm(zb, qt, kb, extra, first_stop):
                    fns = [lambda e: e.matmul(zb[:], lhsT=kT[0:64, kb * 128:(kb + 1) * 128],
                                              rhs=qT[0:64, qt * 512:(qt + 1) * 512], start=True, stop=first_stop)]
                    return fns

                def stageA(i):
                    qt, idx, kb, nk = steps[i]
                    zb = z1s[i % 2]
                    diag = kb >= 4 * qt
                    fns = [lambda e: e.matmul(zb[:], lhsT=kT[:, kb * 128:(kb + 1) * 128],
                                              rhs=qT[:, qt * 512:(qt + 1) * 512], start=True, stop=not diag)]
                    if diag:
                        c = kb - 4 * qt
                        fns.append(lambda e: e.matmul(zb[:], lhsT=ident,
                                                      rhs=cb[:, C_MASKC + c * 512:C_MASKC + (c + 1) * 512],
                                                      start=False, stop=True))
                    S.pe_group(fns, reads=[qT, kT, cb], writes=[zb])

                def stageB(i):
                    qt, idx, kb, nk = steps[i]
                    zb, et, lb = z1s[i % 2], ets[i % 2], lbs[i % 3]
                    S.act(lambda e: e.activation(out=et[:], in_=zb[:], func=AF.Exp), reads=[zb], writes=[et])
                    S.act(lambda e: e.activation(out=lb[:], in_=et[:], func=AF.Ln, bias=1.0), reads=[et], writes=[lb])
                    if idx < nk - 1:
                        if idx == 0:
                            S.pool(lambda e: e.tensor_copy(out=lsf[:], in_=lb[:]), reads=[lb], writes=[lsf])
                        else:
                            S.pool(lambda e: e.tensor_tensor(out=lsf[:], in0=lsf[:], in1=lb[:], op=ALU.add),
                                   reads=[lb, lsf], writes=[lsf])
                        nb_ = lsb[(i + 1) % 4]
                        S.dve(lambda e: e.tensor_copy(out=nb_[:], in_=lsf[:]), reads=[lsf], writes=[nb_])

                def stageC(i):
                    qt, idx, kb, nk = steps[i]
                    zb, lb = z2s[i % 2], lbs[i % 3]
                    diag = kb >= 4 * qt
                    fns = [lambda e: e.matmul(zb[:], lhsT=kT[:, kb * 128:(kb + 1) * 128],
                                              rhs=qT[:, qt * 512:(qt + 1) * 512], start=True, stop=False)]
                    if diag:
                        c = kb - 4 * qt
                        fns.append(lambda e: e.matmul(zb[:], lhsT=ident,
                                                      rhs=cb[:, C_MASKC + c * 512:C_MASKC + (c + 1) * 512],
                                                      start=False, stop=False))
                    fns.append(lambda e: e.matmul(zb[:], lhsT=cb[:, C_TRI:C_TRI + 128], rhs=lb[:], start=False,
                                                  stop=(idx == 0)))
                    rd = [qT, kT, cb, lb]
                    if idx > 0:
                        cur = lsb[i % 4]
                        fns.append(lambda e: e.matmul(zb[:], lhsT=cb[:, C_ONES:C_ONES + 128], rhs=cur[:],
                                                      start=False, stop=True))
                        rd.append(cur)
                    S.pe_group(fns, reads=rd, writes=[zb])

                def stageD(i):
                    zb, ab = z2s[i % 2], abs_[i % 2]
                    S.act(lambda e: e.activation(out=ab[:], in_=zb[:], func=AF.Exp), reads=[zb], writes=[ab])

                def stageE(i):
                    qt, idx, kb, nk = steps[i]
                    ab, ot = abs_[i % 2], ots[qt % 2]
                    S.pe(lambda e: e.matmul(ot[0:64, :], lhsT=vx[:, kb, 0:64], rhs=ab[:], start=(idx == 0),
                                            stop=(idx == nk - 1)), reads=[ab, vx], writes=[ot])
                    if idx == nk - 1:
                        ob = osb.next()
                        S.dve(lambda e: e.tensor_copy(out=ob[:], in_=ot[0:64, :]), reads=[ot], writes=[ob])
                        row = 768 + h * 64
                        S.dma("sp", self.mixT.ap()[row:row + 64, qt * 512:(qt + 1) * 512], ob[:], reads=[ob])

                for t in range(n + 3):
                    if t < n:
                        stageA(t)
                        stageB(t)
                    if 0 <= t - 2 < n:
                        stageC(t - 2)
                        stageD(t - 2)
                    if 0 <= t - 3 < n:
                        stageE(t - 3)

    def att_b(self, l):
        S = self.S
        SL, NT, NB = self.SL, self.NT, self.NB
        W = self.w
        cb = self.cb
        lam_init = 0.8 - 0.6 * math.exp(-0.3 * l)
        with contextlib.ExitStack() as st:
            qts = [S.sbuf(st, "bq%d" % i, [128, SL], BF16) for i in range(2)]
            kts = [[S.sbuf(st, "bk%d_%d" % (i, m), [128, SL], BF16) for m in range(2)] for i in range(2)]
            for t_ in qts:
                S.pool(lambda e, t_=t_: e.memset(t_[64:128, :], 0.0), writes=[t_])
            for i in range(2):
                S.pool(lambda e, i=i: e.memset(kts[i][0][32:64, :], 0.0), writes=[kts[i][0]])
                S.pool(lambda e, i=i: e.memset(kts[i][0][64:128, :], 0.0), writes=[kts[i][0]])
                S.pool(lambda e, i=i: e.memset(kts[i][1][0:32, :], 0.0), writes=[kts[i][1]])
                S.pool(lambda e, i=i: e.memset(kts[i][1][64:128, :], 0.0), writes=[kts[i][1]])
            vxs = [S.sbuf(st, "bv%d" % i, [128, NB, 128], BF16) for i in range(2)]
            tbf = S.sbuf(st, "btbf", [128, WT_B], F32)
            tbs = [S.sbuf(st, "btb%d" % i, [128, WT_B], BF16) for i in range(2)]
            pbs = [S.sbuf(st, "bpb%d" % i, [128, 1024], BF16) for i in range(3)]
            ident = cb[:, C_IDENT:C_IDENT + 128]
            s1s = [S.psum(st, "bs1_%d" % i, [128, 1024], F32) for i in range(2)]
            obs = [[S.psum(st, "bo%d_%d" % (i, m), [128, 512], F32) for m in range(2)] for i in range(2)]
            lam4 = S.sbuf(st, "blam4", [128, 4, 32], F32)
            for i, nm in enumerate(["lambda_q1", "lambda_k1", "lambda_q2", "lambda_k2"]):
                src = bass.AP(tensor=W[nm], offset=l * 32, ap=[[0, 128], [1, 32]])
                S.dma("sp", lam4[:, i, :], src, writes=[lam4])
            lp = S.sbuf(st, "blp", [128, 2, 32], F32)
            S.dve(lambda e: e.tensor_tensor(out=lp[:, 0, :], in0=lam4[:, 0, :], in1=lam4[:, 1, :], op=ALU.mult),
                  reads=[lam4], writes=[lp])
            S.dve(lambda e: e.tensor_tensor(out=lp[:, 1, :], in0=lam4[:, 2, :], in1=lam4[:, 3, :], op=ALU.mult),
                  reads=[lam4], writes=[lp])
            ls = S.sbuf(st, "bls", [128, 2], F32)
            S.dve(lambda e: e.tensor_reduce(out=ls[:], in_=lp[:], axis=mybir.AxisListType.X, op=ALU.add),
                  reads=[lp], writes=[ls])
            le = S.sbuf(st, "ble", [128, 2], F32)
            S.act(lambda e: e.activation(out=le[:], in_=ls[:], func=AF.Exp), reads=[ls], writes=[le])
            neglam = S.sbuf(st, "bneglam", [128, 1], F32)
            S.dve(lambda e: e.tensor_tensor(out=neglam[:], in0=le[:, 1:2], in1=le[:, 0:1], op=ALU.subtract),
                  reads=[le], writes=[neglam])
            S.dve(lambda e: e.tensor_scalar(out=neglam[:], in0=neglam[:], scalar1=-lam_init, scalar2=None,
                                            op0=ALU.add), reads=[neglam], writes=[neglam])
            gsub = S.sbuf(st, "bgsub", [128, 1], F32)
            src = bass.AP(tensor=W["diff_subln"], offset=l * 64, ap=[[1, 64], [1, 1]])
            S.dma("sp", gsub[0:64, :], src, writes=[gsub])
            S.dve(lambda e: e.tensor_scalar(out=gsub[0:64, :], in0=gsub[0:64, :], scalar1=1.0 - lam_init,
                                            scalar2=None, op0=ALU.mult), reads=[gsub], writes=[gsub])
            rts = [S.sbuf(st, "brt%d" % i, [64, 512], F32) for i in range(2)]
            nts = [S.sbuf(st, "bnt%d" % i, [64, 512], F32) for i in range(2)]
            dts = S.sbuf(st, "bdt", [64, 512], F32)
            sqd = S.sbuf(st, "bsqd", [64, 512], BF16)
            lnm = S.sbuf(st, "blnm", [64, 512], F32)
            rsm = S.sbuf(st, "brsm", [64, 512], F32)
            osb = Rot([S.sbuf(st, "bos%d" % i, [64, 512], BF16) for i in range(2)])

            def load_b(h, slot):
                qrow, krow = 1024 + h * 64, 1280 + h * 64
                S.dma("sp", qts[slot][0:64, :], self.qk.ap()[qrow:qrow + 64, :], writes=[qts[slot]])
                S.dma("sp", kts[slot][0][0:32, :], self.qk.ap()[krow:krow + 32, :], writes=[kts[slot][0]])
                S.dma("sp", kts[slot][1][32:64, :], self.qk.ap()[krow + 32:krow + 64, :], writes=[kts[slot][1]])
                src = self.vx.ap()[:, (8 + h) * 128:(9 + h) * 128].rearrange("(n p) e -> p n e", p=128)
                S.dma("sp", vxs[slot][:], src, writes=[vxs[slot]])
                S.dma("sp", tbf[:], self.tb_d.ap()[h], writes=[tbf])
                S.dve(lambda e: e.tensor_copy(out=tbs[slot][:], in_=tbf[:]), reads=[tbf], writes=[tbs[slot]])

            for h in range(4):
                if h == 0:
                    load_b(0, 0)
                if h + 1 < 4:
                    load_b(h + 1, (h + 1) % 2)
                qT, kTm, vx, tb = qts[h % 2], kts[h % 2], vxs[h % 2], tbs[h % 2]
                steps = []
                for qt in range(NT):
                    nk = 4 * qt + 4
                    for kb in range(nk):
                        steps.append((qt, kb, nk))
                n = len(steps)

                def stageA(i):
                    qt, kb, nk = steps[i]
                    sb_ = s1s[i % 2]
                    Dd = qt * 512 - kb * 128
                    near = Dd < FAR_D
                    fns = []
                    for m in range(2):
                        kT = kTm[m]
                        fns.append(lambda e, m=m, kT=kT: e.matmul(
                            sb_[:, m * 512:(m + 1) * 512], lhsT=kT[:, kb * 128:(kb + 1) * 128],
                            rhs=qT[:, qt * 512:(qt + 1) * 512], start=True, stop=not near))
                        if near:
                            c0 = Dd + 384
                            fns.append(lambda e, m=m, c0=c0: e.matmul(
                                sb_[:, m * 512:(m + 1) * 512], lhsT=ident, rhs=tb[:, c0:c0 + 512], start=False,
                                stop=True))
                    S.pe_group(fns, reads=[qT, kTm[0], kTm[1], tb, cb], writes=[sb_])

                def stageB(i):
                    s_, pb = s1s[i % 2], pbs[i % 3]
                    S.act(lambda e: e.activation(out=pb[:], in_=s_[:], func=AF.Exp), reads=[s_], writes=[pb])

                def stageC(i):
                    qt, kb, nk = steps[i]
                    pb = pbs[i % 3]
                    o0, o1 = obs[qt % 2]
                    pmb = o0
                    S.pe_group([lambda e, m=m, ob=ob: e.matmul(ob[:], lhsT=vx[:, kb, :], rhs=pb[:, m * 512:(m + 1) * 512],
                                                               start=(kb == 0), stop=(kb == nk - 1))
                                for m, ob in ((0, o0), (1, o1))], reads=[pb, vx], writes=[o0, o1])
                    if kb == nk - 1:
                        o0, o1 = obs[qt % 2]
                        for mm_, o_ in ((0, o0), (1, o1)):
                            rt, nt = rts[mm_], nts[mm_]
                            S.dve(lambda e, o_=o_, rt=rt: e.reciprocal(out=rt[:], in_=o_[64:128, :]),
                                  reads=[o_], writes=[rt])
                            S.dve(lambda e, o_=o_, rt=rt, nt=nt: e.tensor_tensor(out=nt[:], in0=o_[0:64, :], in1=rt[:],
                                                                                 op=ALU.mult),
                                  reads=[o_, rt], writes=[nt])
                        S.dve(lambda e: e.scalar_tensor_tensor(out=dts[:], in0=nts[1][:], scalar=neglam[0:64, :],
                                                               in1=nts[0][:], op0=ALU.mult, op1=ALU.add),
                              reads=[nts[0], nts[1], neglam], writes=[dts])
                        S.act(lambda e: e.activation(out=sqd[:], in_=dts[:], func=AF.Square), reads=[dts], writes=[sqd])
                        S.pe(lambda e: e.matmul(pmb[0:64, :], lhsT=cb[0:64, C_BD64:C_BD64 + 64], rhs=sqd[:],
                                                start=True, stop=True), reads=[sqd, cb], writes=[pmb])
                        S.act(lambda e: e.activation(out=lnm[:], in_=pmb[0:64, :], func=AF.Ln, bias=EPS),
                              reads=[pmb], writes=[lnm])
                        S.act(lambda e: e.activation(out=rsm[:], in_=lnm[:], func=AF.Exp, scale=-0.5),
                              reads=[lnm], writes=[rsm])
                        ob_ = osb.next()
                        S.dve(lambda e: e.scalar_tensor_tensor(out=ob_[:], in0=dts[:], scalar=gsub[0:64, :],
                                                               in1=rsm[:], op0=ALU.mult, op1=ALU.mult),
                              reads=[dts, rsm, gsub], writes=[ob_])
                        row = 512 + h * 64
                        S.dma("sp", self.mixT.ap()[row:row + 64, qt * 512:(qt + 1) * 512], ob_[:], reads=[ob_])

                for t in range(n + 2):
                    if t < n:
                        stageA(t)
                    if 0 <= t - 1 < n:
                        stageB(t - 1)
                    if 0 <= t - 2 < n:
                        stageC(t - 2)

    def att_a(self, l):
        S = self.S
        SL, NT, NB = self.SL, self.NT, self.NB
        with contextlib.ExitStack() as st:
            qts = [S.sbuf(st, "aq%d" % i, [128, SL], BF16) for i in range(2)]
            kts = [S.sbuf(st, "ak%d" % i, [128, SL], BF16) for i in range(2)]
            for t_ in qts + kts:
                S.pool(lambda e, t_=t_: e.memset(t_[64:128, :], 0.0), writes=[t_])
            vxs = [S.sbuf(st, "av%d" % i, [128, NB, 128], BF16) for i in range(2)]
            taf = S.sbuf(st, "ataf", [128, 3 * 256], F32)
            tas = [S.sbuf(st, "ata%d" % i, [128, 3 * 256], BF16) for i in range(2)]
            ident = self.cb[:, C_IDENT:C_IDENT + 128]
            acc = S.sbuf(st, "aacc", [128, SL], F32)
            pbs = [S.sbuf(st, "apb%d" % i, [128, 256], BF16) for i in range(3)]
            s1s = [S.psum(st, "as1_%d" % i, [128, 512], F32) for i in range(3)]
            regs = [S.psum(st, "areg%d" % i, [128, 512], F32) for i in range(4)]
            regb = [Buf("aregb%d" % i) for i in range(16)]
            rct = Rot([S.sbuf(st, "arc%d" % i, [64, 512], F32) for i in range(2)])
            osb = Rot([S.sbuf(st, "aos%d" % i, [64, 512], BF16) for i in range(2)])
            vcount = [0]

            def load_v(h, bi):
                w, dil = BRANCHES[bi]
                nb = SL // dil // 128
                vx_ = vxs[vcount[0] % 2]
                vcount[0] += 1
                for r in range(dil):
                    src = bass.AP(tensor=self.vx, offset=r * 2048 + h * 128,
                                  ap=[[dil * 2048, 128], [128 * dil * 2048, nb], [1, 128]])
                    S.dma("sp", vx_[:, r * nb:(r + 1) * nb, :], src, writes=[vx_])
                return vx_

            def load_qk(h, slot):
                self.load_head(qts[slot], kts[slot], None, h * 64, 512 + h * 64, None)
                S.dma("sp", taf[:], self.ta_d.ap()[h], writes=[taf])
                S.dve(lambda e: e.tensor_copy(out=tas[slot][:], in_=taf[:]), reads=[taf], writes=[tas[slot]])

            qid = [0]
            for h in range(8):
                if h == 0:
                    load_qk(0, 0)
                if h + 1 < 8:
                    load_qk(h + 1, (h + 1) % 2)
                qT, kT, ta = qts[h % 2], kts[h % 2], tas[h % 2]
                for bi, (w, dil) in enumerate(BRANCHES):
                    nb = SL // dil // 128
                    vx = load_v(h, bi)
                    S._wait("dve", ("dve", S.cnt["dve"]))
                    steps = [(r, n_) for r in range(dil) for n_ in range(nb)]
                    n = len(steps)
                    info = {}

                    def tok(r, n_, cnt):
                        start = (n_ * 128) * dil + r
                        return slice(start, start + (cnt - 1) * dil + 1, dil)

                    def stageA(i):
                        r, n_ = steps[i]
                        nq = 256 if n_ + 1 < nb else 128
                        sb_ = s1s[i % 3]
                        S.pe_group([
                            lambda e: e.matmul(sb_[:, 0:nq], lhsT=kT[:, tok(r, n_, 128)], rhs=qT[:, tok(r, n_, nq)],
                                               start=True, stop=False),
                            lambda e: e.matmul(sb_[:, 0:nq], lhsT=ident, rhs=ta[:, bi * 256:bi * 256 + nq],
                                               start=False, stop=True)],
                            reads=[qT, kT, ta, self.cb], writes=[sb_])

                    def stageB(i):
                        r, n_ = steps[i]
                        nq = 256 if n_ + 1 < nb else 128
                        s_, pb = s1s[i % 3], pbs[i % 3]
                        S.act(lambda e: e.activation(out=pb[:, 0:nq], in_=s_[:, 0:nq], func=AF.Exp),
                              reads=[s_], writes=[pb])

                    g = min(4, nb)

                    def stageC(i):
                        r, n_ = steps[i]
                        pb = pbs[i % 3]
                        blk = r * nb + n_
                        if (r, n_ // g) not in info:
                            info[(r, n_ // g)] = qid[0]
                            qid[0] += 1
                        bk = info[(r, n_ // g)] % 4
                        rg, rb_ = regs[bk], regb[bk]
                        cs = (n_ % g) * 128
                        S.pe(lambda e: e.matmul(rg[:, cs:cs + 128], lhsT=vx[:, blk, :], rhs=pb[:, 0:128],
                                                start=(n_ == 0), stop=True), reads=[pb, vx], writes=[rb_])
                        if n_ % g == g - 1:
                            tsl = tok(r, n_ - (g - 1), g * 128)
                            if bi == 0:
                                S.dve(lambda e: e.tensor_copy(out=acc[:, tsl], in_=rg[:, 0:g * 128]),
                                      reads=[rb_], writes=[])
                            else:
                                S.dve(lambda e: e.tensor_tensor(out=acc[:, tsl], in0=rg[:, 0:g * 128], in1=acc[:, tsl],
                                                                op=ALU.add), reads=[rb_], writes=[])
                        if n_ + 1 < nb:
                            if (n_ + 1) % g == 0:
                                info[(r, (n_ + 1) // g)] = qid[0]
                                qid[0] += 1
                            bk2 = info[(r, (n_ + 1) // g)] % 4
                            rg2 = regs[bk2]
                            cs2 = ((n_ + 1) % g) * 128
                            S.pe(lambda e: e.matmul(rg2[:, cs2:cs2 + 128], lhsT=vx[:, blk, :], rhs=pb[:, 128:256],
                                                    start=True, stop=False), reads=[pb, vx], writes=[regb[bk2]])

                    for t in range(n + 2):
                        if t < n:
                            stageA(t)
                        if 0 <= t - 1 < n:
                            stageB(t - 1)
                        if 0 <= t - 2 < n:
                            stageC(t - 2)
                S._wait("dve", ("dve", S.cnt["dve"]))
                accb = Buf("accb")
                for ci in range(SL // 512):
                    rc = rct.next()
                    ob = osb.next()
                    S.dve(lambda e: e.reciprocal(out=rc[:], in_=acc[64:128, ci * 512:(ci + 1) * 512]),
                          reads=[accb], writes=[rc])
                    S.dve(lambda e: e.tensor_tensor(out=ob[:], in0=acc[0:64, ci * 512:(ci + 1) * 512], in1=rc[:],
                                                    op=ALU.mult), reads=[accb, rc], writes=[ob])
                    S.dma("sp", self.mixT.ap()[h * 64:(h + 1) * 64, ci * 512:(ci + 1) * 512], ob[:], reads=[ob])


_CACHE = {}


def get_prog(S_len, depth, dbg=False, phases=None):
    key = (S_len, depth, dbg, tuple(phases) if phases else None)
    if key not in _CACHE:
        p = Prog(S_len, depth, dbg, phases)
        p.build()
        _CACHE[key] = p
    return _CACHE[key]


def kernel(**inputs):
    x = np.asarray(inputs["x"], dtype=np.float32)
    B, S_len, _ = x.shape
    depth = int(np.asarray(inputs["w_in"]).shape[0])
    prog = get_prog(S_len, depth)
    consts, onehot = make_consts()
    shared = {k: np.ascontiguousarray(np.asarray(v, dtype=np.float32)) for k, v in inputs.items() if k != "x"}
    shared["consts"] = consts
    shared["onehot"] = onehot
    in_maps = []
    for b in range(B):
        m = dict(shared)
        m["x"] = np.ascontiguousarray(x[b])
        in_maps.append(m)
    res = run_bass_kernel_spmd(prog.nc, in_maps, core_ids=list(range(B)))
    return np.stack([np.asarray(r["y"]) for r in res.results], axis=0).astype(np.float32)
```

```python
import contextlib
import math
import numpy as np
import concourse.bass as bass
import concourse.mybir as mybir
from concourse.bass_utils import run_bass_kernel_spmd

F32 = mybir.dt.float32
BF16 = mybir.dt.bfloat16
AF = mybir.ActivationFunctionType
ALU = mybir.AluOpType

D = 1024
HD = 64
N_BUCKETS = 32
MAX_DISTANCE = 2048
EPS = 1e-6
NEGM = -30000.0
BRANCHES = ((128, 1), (512, 4), (2048, 16))
N_DMA_SLOTS = 8

QA0, KA0, VA0, QB0, KB0, VB0, QC0, KC0, VC0 = 0, 512, 1024, 1536, 1792, 2048, 2304, 2560, 2816


def t5_bucket_np(dist):
    dist = np.maximum(dist, 0)
    max_exact = N_BUCKETS // 2
    d_f = np.maximum(dist, 1).astype(np.float32)
    large = max_exact + (np.log(d_f / np.float32(max_exact)) / np.float32(math.log(MAX_DISTANCE / max_exact))
                         * np.float32(N_BUCKETS - max_exact)).astype(np.int32)
    large = np.minimum(large, N_BUCKETS - 1)
    return np.where(dist < max_exact, dist, large)


def _far_d():
    d = 0
    while True:
        if np.all(t5_bucket_np(np.arange(max(d - 127, 0), d + 1024)) == N_BUCKETS - 1) and d - 127 >= 0:
            return d
        d += 128


FAR_D = _far_d()
WT_B = FAR_D + 384 + 512 - 128
UB = WT_B + 127
UA = 256 + 127
UTOT = UB + 3 * UA


def make_consts():
    c = {}
    c["ident"] = np.eye(128, dtype=np.float32)
    c["jflip"] = np.eye(128, dtype=np.float32)[::-1].copy()
    j = np.arange(128)[:, None]
    s = np.arange(128)[None, :]
    c["trineg"] = -(j >= s).astype(np.float32)
    c["onesneg"] = -np.ones((128, 128), np.float32)
    bd64 = np.zeros((128, 128), np.float32)
    for g in range(2):
        bd64[g * 64:(g + 1) * 64, g * 64:(g + 1) * 64] = 1.0 / 64
    bd32 = np.zeros((128, 128), np.float32)
    for g in range(4):
        bd32[g * 32:(g + 1) * 32, g * 32:(g + 1) * 32] = 1.0 / 32
    c["bd64"] = bd64
    c["bd32"] = bd32
    mc = np.zeros((128, 4, 4, 128), np.float32)
    for cc in range(4):
        for qs in range(4):
            if qs < cc:
                mc[:, cc, qs, :] = -1e4
            elif qs == cc:
                mc[:, cc, qs, :] = np.where(j < s, 0.0, -1e4)
    c["maskc"] = mc.reshape(128, 4 * 512)
    consts = np.concatenate([c[k] for k in ["ident", "jflip", "trineg", "onesneg", "bd64", "bd32", "maskc"]], axis=1)
    oh = np.zeros((33, UTOT), np.float32)
    u = np.arange(UB)
    dist = u - 511
    b = t5_bucket_np(dist)
    oh[b[dist >= 0], u[dist >= 0]] = 1.0
    oh[32, u[dist < 0]] = 1.0
    for bi, (w, dil) in enumerate(BRANCHES):
        u = np.arange(UA)
        off = u - 127
        ok = (off >= 0) & (off <= 128)
        b = t5_bucket_np(off * dil)
        base = UB + bi * UA
        oh[b[ok], base + u[ok]] = 1.0
        oh[32, base + u[~ok]] = 1.0
    return np.ascontiguousarray(consts), oh


C_IDENT, C_JFLIP, C_TRI, C_ONES, C_BD64, C_BD32, C_MASKC = 0, 128, 256, 384, 512, 640, 768
C_TOT = 768 + 2048


class Buf:
    __slots__ = ("name", "last_w", "readers", "t")

    def __init__(self, name, t=None):
        self.name = name
        self.last_w = None
        self.readers = []
        self.t = t

    def __getitem__(self, idx):
        return self.t[idx]


class Sched:
    def __init__(self, nc, stack):
        self.nc = nc
        self.eng = {"pe": nc.tensor, "act": nc.scalar, "dve": nc.vector, "pool": nc.gpsimd, "sp": nc.sync}
        self.sem = {}
        self.cnt = {}
        for e in ("pe", "act", "dve", "pool"):
            self.sem[e] = stack.enter_context(nc.semaphore("s_" + e))
            self.cnt[e] = 0
        self.dq = {}
        for q in ("sp", "act", "pool"):
            for s in range(N_DMA_SLOTS):
                k = ("dma", q, s)
                self.sem[k] = stack.enter_context(nc.semaphore("d_%s%d" % (q, s)))
                self.cnt[k] = 0
            self.dq[q] = 0
        self.waited = {}
        self.nops = 0

    def sbuf(self, stack, name, shape, dtype):
        self.uid = getattr(self, "uid", 0) + 1
        name = "%s_%d" % (name, self.uid)
        return Buf(name, stack.enter_context(self.nc.sbuf_tensor(name, list(shape), dtype)))

    def psum(self, stack, name, shape, dtype):
        self.uid = getattr(self, "uid", 0) + 1
        name = "%s_%d" % (name, self.uid)
        return Buf(name, stack.enter_context(self.nc.psum_tensor(name, list(shape), dtype)))

    def _wait(self, e, ticket):
        if ticket is None:
            return
        k, v = ticket
        key = (e, k)
        if self.waited.get(key, 0) >= v:
            return
        self.waited[key] = v
        self.eng[e].wait_ge(self.sem[k], v)

    def _deps(self, e, reads, writes):
        for b in reads:
            self._wait(e, b.last_w)
        for b in writes:
            self._wait(e, b.last_w)
            for r in b.readers:
                self._wait(e, r)

    def _commit(self, ticket, reads, writes):
        for b in reads:
            b.readers.append(ticket)
        for b in writes:
            b.last_w = ticket
            b.readers = []

    def op(self, e, fn, reads=(), writes=()):
        self._deps(e, reads, writes)
        ins = fn(self.eng[e])
        self.cnt[e] += 1
        ins.then_inc(self.sem[e], 1)
        self._commit((e, self.cnt[e]), reads, writes)
        self.nops += 1
        return ins

    def pe(self, fn, reads=(), writes=()):
        return self.op("pe", fn, reads, writes)

    def act(self, fn, reads=(), writes=()):
        return self.op("act", fn, reads, writes)

    def dve(self, fn, reads=(), writes=()):
        return self.op("dve", fn, reads, writes)

    def pool(self, fn, reads=(), writes=()):
        return self.op("pool", fn, reads, writes)

    def pe_group(self, fns, reads=(), writes=()):
        self._deps("pe", reads, writes)
        ins = None
        for fn in fns:
            ins = fn(self.eng["pe"])
        self.cnt["pe"] += 1
        ins.then_inc(self.sem["pe"], 1)
        self._commit(("pe", self.cnt["pe"]), reads, writes)
        self.nops += len(fns)
        return ins

    def dma(self, q, out, in_, reads=(), writes=(), **kw):
        s = self.dq[q] % N_DMA_SLOTS
        self.dq[q] += 1
        k = ("dma", q, s)
        if self.cnt[k] > 0:
            self._wait(q, (k, self.cnt[k]))
        self._deps(q, reads, writes)
        ins = self.eng[q].dma_start(out=out, in_=in_, **kw)
        self.cnt[k] += 16
        ins.then_inc(self.sem[k], 16)
        self._commit((k, self.cnt[k]), reads, writes)
        self.nops += 1
        return ins

    def barrier(self):
        tickets = [(k, v) for k, v in self.cnt.items() if v > 0]
        for e in ("pe", "act", "dve", "pool", "sp"):
            for t in tickets:
                self._wait(e, t)


class Rot:
    def __init__(self, items):
        self.items = items
        self.i = 0

    def next(self):
        b = self.items[self.i % len(self.items)]
        self.i += 1
        return b


class Prog:
    def __init__(self, S_len, depth, dbg=False, phases=None):
        self.SL = S_len
        self.depth = depth
        self.NT = S_len // 512
        self.NB = S_len // 128
        self.dbg = dbg
        self.phases = phases
        nc = bass.Bass("TRN2", target_bir_lowering=False)
        self.nc = nc
        ein = "ExternalInput"
        dt = nc.dram_tensor
        self.x = dt("x", [S_len, D], F32, kind=ein)
        self.rel_bias = dt("rel_bias", [32, 12], F32, kind=ein)
        L = depth
        self.w = {}
        for n, shp in [("ffn1_norm", [L, D]), ("ffn1_w_gate", [L, D, D]), ("ffn1_w_up", [L, D, D]),
                       ("ffn1_w_down", [L, D, D]), ("mix_norm", [L, D]), ("w_in", [L, D, 3 * D]),
                       ("q_norm_a", [L, 64]), ("k_norm_a", [L, 64]), ("q_norm_b", [L, 32]), ("k_norm_b", [L, 32]),
                       ("lambda_q1", [L, 32]), ("lambda_k1", [L, 32]), ("lambda_q2", [L, 32]), ("lambda_k2", [L, 32]),
                       ("diff_subln", [L, 64]), ("w_out", [L, D, D]), ("ffn2_norm", [L, D]),
                       ("ffn2_w_gate", [L, D, D]), ("ffn2_w_up", [L, D, D]), ("ffn2_w_down", [L, D, D])]:
            self.w[n] = dt(n, shp, F32, kind=ein)
        self.consts_d = dt("consts", [128, C_TOT], F32, kind=ein)
        self.onehot_d = dt("onehot", [33, UTOT], F32, kind=ein)
        self.y = dt("y", [S_len, D], F32, kind="ExternalOutput")
        sk = "ExternalOutput" if dbg else "Internal"
        self.xs = dt("xs_s", [S_len, D], F32, kind=sk)
        self.qk = dt("qk_s", [2048, S_len], BF16, kind=sk)
        self.vx = dt("vx_s", [S_len, 16 * 128], BF16, kind=sk)
        self.mixT = dt("mix_s", [D, S_len], BF16, kind=sk)
        self.tabs = dt("tabs_s", [12, UTOT], F32, kind=sk)
        self.tb_d = dt("tb_s", [4, 128, WT_B], F32, kind=sk)
        self.ta_d = dt("ta_s", [8, 128, 3 * 256], F32, kind=sk)

    def build(self):
        nc = self.nc
        with contextlib.ExitStack() as top:
            S = Sched(nc, top)
            self.S = S
            self.cb = S.sbuf(top, "cb", [128, C_TOT], BF16)
            self.cf = S.sbuf(top, "cf", [128, 256], F32)
            self.mhalf = S.sbuf(top, "mhalf", [128, 8], F32)
            self.setup_consts()
            ph = self.phases
            for l in range(self.depth):
                if ph is None or "p1" in ph or ("%d:p1" % l) in ph:
                    self.phase_p1(l)
                    S.barrier()
                if ph is None or "att" in ph or ("%d:att" % l) in ph:
                    self.phase_att(l)
                    S.barrier()
                if ph is None or "p3" in ph or ("%d:p3" % l) in ph:
                    self.phase_p3(l)
                    S.barrier()
            S.barrier()
        return nc

    def setup_consts(self):
        S = self.S
        with contextlib.ExitStack() as st:
            stg = S.sbuf(st, "c_stg", [128, C_TOT], F32)
            S.dma("sp", stg[:], self.consts_d.ap(), writes=[stg])
            S.dve(lambda e: e.tensor_copy(out=self.cb[:], in_=stg[:]), reads=[stg], writes=[self.cb])
            S.dve(lambda e: e.tensor_copy(out=self.cf[:], in_=stg[:, 0:256]), reads=[stg], writes=[self.cf])
            S.pool(lambda e: e.memset(self.mhalf[:], -0.5), writes=[self.mhalf])
            rbx = S.sbuf(st, "rbx", [33, 12], F32)
            S.dve(lambda e: e.memset(rbx[:], NEGM), writes=[rbx])
            S.dma("sp", rbx[0:32, :], self.rel_bias.ap(), writes=[rbx])
            oh = S.sbuf(st, "oh", [33, UTOT], F32)
            S.dma("sp", oh[:], self.onehot_d.ap(), writes=[oh])
            tsb = S.sbuf(st, "tsb", [12, UTOT], F32)
            pst = S.psum(st, "pst", [128, 512], F32)
            c0 = 0
            while c0 < UTOT:
                w = min(512, UTOT - c0)
                S.pe(lambda e, c0=c0, w=w: e.matmul(pst[0:12, 0:w], lhsT=rbx[:, :], rhs=oh[:, c0:c0 + w],
                                                   start=True, stop=True), reads=[rbx, oh], writes=[pst])
                S.dve(lambda e, c0=c0, w=w: e.tensor_copy(out=tsb[:, c0:c0 + w], in_=pst[0:12, 0:w]),
                      reads=[pst], writes=[tsb])
                c0 += w
            cfc = S.sbuf(st, "cfc", [12, 1], F32)
            S.dve(lambda e: e.tensor_copy(out=cfc[:], in_=tsb[:, UB - 1:UB]), reads=[tsb], writes=[cfc])
            S.dve(lambda e: e.tensor_scalar(out=tsb[:, 0:UB], in0=tsb[:, 0:UB], scalar1=cfc[:, 0:1], scalar2=None,
                                            op0=ALU.subtract), reads=[tsb, cfc], writes=[tsb])
            S.dma("sp", self.tabs.ap(), tsb[:], reads=[tsb])
            S.barrier()
            gp = S.sbuf(st, "gp", [128, WT_B], F32)
            tt = S.sbuf(st, "tt", [128, WT_B], F32)

            def expand(row, u0, width, dst_ap):
                src = bass.AP(tensor=self.tabs, offset=row * UTOT + u0, ap=[[1, 128], [1, width]])
                S.dma("sp", gp[:, 0:width], src, writes=[gp])
                c = 0
                while c < width:
                    w = min(512, width - c)
                    S.pe(lambda e, c=c, w=w: e.matmul(pst[:, 0:w], lhsT=self.cf[:, 128:256], rhs=gp[:, c:c + w],
                                                     start=True, stop=True), reads=[gp, self.cf], writes=[pst])
                    S.dve(lambda e, c=c, w=w: e.tensor_copy(out=tt[:, c:c + w], in_=pst[:, 0:w]),
                          reads=[pst], writes=[tt])
                    c += w
                S.dma("sp", dst_ap, tt[:, 0:width], reads=[tt])

            for h in range(4):
                expand(8 + h, 0, WT_B, self.tb_d.ap()[h])
            for h in range(8):
                for bi in range(3):
                    expand(h, UB + bi * UA, 256, self.ta_d.ap()[h][:, bi * 256:(bi + 1) * 256])
            S.barrier()

    def load_weight(self, st_pool, wap, dst, ncols, gcol=None, cscale=None):
        S = self.S
        for k in range(8):
            for c0 in range(0, ncols, 1024):
                stg = st_pool.next()
                S.dma("sp", stg[:], wap[k * 128:(k + 1) * 128, c0:c0 + 1024], writes=[stg])
                use_act = (self._wl % 2 == 1)
                self._wl += 1
                if use_act:
                    sc = gcol[:, k:k + 1] if gcol is not None else float(cscale or 1.0)
                    S.act(lambda e, k=k, c0=c0, stg=stg, sc=sc: e.activation(
                        out=dst[:, k, c0:c0 + 1024], in_=stg[:], func=AF.Copy, scale=sc),
                        reads=[stg] + ([gcol] if gcol is not None else []), writes=[dst])
                elif gcol is not None:
                    S.dve(lambda e, k=k, c0=c0, stg=stg: e.tensor_scalar(
                        out=dst[:, k, c0:c0 + 1024], in0=stg[:], scalar1=gcol[:, k:k + 1], scalar2=None,
                        op0=ALU.mult), reads=[stg, gcol], writes=[dst])
                else:
                    S.dve(lambda e, k=k, c0=c0, stg=stg: e.tensor_scalar(
                        out=dst[:, k, c0:c0 + 1024], in0=stg[:], scalar1=float(cscale or 1.0), scalar2=None,
                        op0=ALU.mult), reads=[stg], writes=[dst])

    def load_gcol(self, dst, vec_ap):
        src = bass.AP(tensor=vec_ap.tensor, offset=vec_ap.offset, ap=[[1, 128], [128, 8]])
        self.S.dma("sp", dst[:], src, writes=[dst], allow_slow_non_contiguous=True)

    def alloc_ffn(self, st):
        S = self.S
        f = {}
        f["sqj"] = S.sbuf(st, "sqj", [128, 1024], BF16)
        f["ss"] = S.sbuf(st, "ss", [128, 4], F32)
        f["ms"] = S.sbuf(st, "ms", [128, 4], F32)
        f["rstd"] = S.sbuf(st, "rstd", [128, 4], F32)
        f["hbf"] = [S.sbuf(st, "hbf%d" % j, [128, 1024], BF16) for j in range(4)]
        f["hT"] = S.sbuf(st, "hT", [128, 8, 512], BF16)
        f["hTp"] = [Buf("hT%d" % i) for i in range(4)]
        f["sg"] = Rot([S.sbuf(st, "sg%d" % i, [128, 512], F32) for i in range(2)])
        f["aT"] = S.sbuf(st, "aT", [128, 8, 512], BF16)
        f["aTp"] = [Buf("aT%d" % i) for i in range(8)]
        return f

    def norm_transpose(self, f, xt, xtp, tps):
        S = self.S
        ss, ms, rstd = f["ss"], f["ms"], f["rstd"]
        for j in range(4):
            S.act(lambda e, j=j: e.activation(out=f["sqj"][:], in_=xt[:, j, :], func=AF.Square,
                                              accum_out=ss[:, j:j + 1]), reads=[xtp[j]], writes=[f["sqj"], ss])
        S.dve(lambda e: e.tensor_scalar(out=ms[:], in0=ss[:], scalar1=1.0 / D, scalar2=EPS, op0=ALU.mult,
                                        op1=ALU.add), reads=[ss], writes=[ms])
        S.pool(lambda e: e.tensor_tensor(out=rstd[:], in0=ms[:], in1=self.mhalf[:, 0:4], op=ALU.pow),
               reads=[ms, self.mhalf], writes=[rstd])
        for j in range(4):
            S.dve(lambda e, j=j: e.tensor_scalar(out=f["hbf"][j][:], in0=xt[:, j, :], scalar1=rstd[:, j:j + 1],
                                                 scalar2=None, op0=ALU.mult),
                  reads=[xtp[j], rstd], writes=[f["hbf"][j]])
        for kk in range(4):
            tp = tps.next()
            fns = []
            for k2 in range(2):
                k = kk * 2 + k2
                for j in range(4):
                    fns.append(lambda e, k=k, k2=k2, j=j, tp=tp: e.transpose(
                        out=tp[:, k2 * 512 + j * 128:k2 * 512 + (j + 1) * 128],
                        in_=f["hbf"][j][:, k * 128:(k + 1) * 128], identity=self.cb[:, C_IDENT:C_IDENT + 128]))
            S.pe_group(fns, reads=f["hbf"] + [self.cb], writes=[tp])
            eng = "act" if kk % 2 == 0 else "dve"
            if eng == "act":
                S.act(lambda e, kk=kk, tp=tp: e.activation(out=f["hT"][:, 2 * kk:2 * kk + 2, :],
                                                           in_=tp[:].rearrange("p (a b) -> p a b", a=2),
                                                           func=AF.Copy), reads=[tp], writes=[f["hTp"][kk]])
            else:
                S.dve(lambda e, kk=kk, tp=tp: e.tensor_copy(out=f["hT"][:, 2 * kk:2 * kk + 2, :],
                                                            in_=tp[:].rearrange("p (a b) -> p a b", a=2)),
                      reads=[tp], writes=[f["hTp"][kk]])

    def ffn_tile(self, f, xt, xtp, wg, wu, wd, mms, tps):
        S = self.S
        self.norm_transpose(f, xt, xtp, tps)
        hT, aT = f["hT"], f["aT"]
        for n in range(8):
            pg = mms.next()
            pu = mms.next()
            S.pe_group([lambda e, k=k, n=n, pg=pg: e.matmul(pg[:], lhsT=wg[:, k, n * 128:(n + 1) * 128],
                                                            rhs=hT[:, k, :], start=(k == 0), stop=(k == 7))
                        for k in range(8)], reads=f["hTp"] + [wg], writes=[pg])
            S.pe_group([lambda e, k=k, n=n, pu=pu: e.matmul(pu[:], lhsT=wu[:, k, n * 128:(n + 1) * 128],
                                                            rhs=hT[:, k, :], start=(k == 0), stop=(k == 7))
                        for k in range(8)], reads=f["hTp"] + [wu], writes=[pu])
            sg = f["sg"].next()
            S.act(lambda e, pg=pg, sg=sg: e.activation(out=sg[:], in_=pg[:], func=AF.Silu), reads=[pg], writes=[sg])
            S.dve(lambda e, n=n, sg=sg, pu=pu: e.tensor_tensor(out=aT[:, n, :], in0=sg[:], in1=pu[:], op=ALU.mult),
                  reads=[sg, pu], writes=[f["aTp"][n]])
        for j in range(4):
            for hf in range(2):
                py = mms.next()
                S.pe_group([lambda e, k=k, j=j, hf=hf, py=py: e.matmul(
                    py[:], lhsT=aT[:, k, j * 128:(j + 1) * 128], rhs=wd[:, k, hf * 512:(hf + 1) * 512],
                    start=(k == 0), stop=(k == 7)) for k in range(8)], reads=f["aTp"] + [wd], writes=[py])
                S.dve(lambda e, j=j, hf=hf, py=py: e.tensor_tensor(
                    out=xt[:, j, hf * 512:(hf + 1) * 512], in0=py[:], in1=xt[:, j, hf * 512:(hf + 1) * 512],
                    op=ALU.add), reads=[py, xtp[j]], writes=[xtp[j]])

    def phase_p1(self, l):
        S = self.S
        W = self.w
        xin = self.x if l == 0 else self.xs
        with contextlib.ExitStack() as st:
            self._wl = 0
            stg = Rot([S.sbuf(st, "wstg%d" % i, [128, 1024], F32) for i in range(2)])
            wg = S.sbuf(st, "wg", [128, 8, 1024], BF16)
            wu = S.sbuf(st, "wu", [128, 8, 1024], BF16)
            wd = S.sbuf(st, "wd", [128, 8, 1024], BF16)
            wi = S.sbuf(st, "wi", [128, 8, 3072], BF16)
            g1 = S.sbuf(st, "g1", [128, 8], F32)
            g2 = S.sbuf(st, "g2", [128, 8], F32)
            self.load_gcol(g1, W["ffn1_norm"].ap()[l])
            self.load_gcol(g2, W["mix_norm"].ap()[l])
            self.load_weight(stg, W["ffn1_w_gate"].ap()[l], wg, 1024, gcol=g1)
            self.load_weight(stg, W["ffn1_w_up"].ap()[l], wu, 1024, gcol=g1)
            self.load_weight(stg, W["ffn1_w_down"].ap()[l], wd, 1024, cscale=0.5)
            self.load_weight(stg, W["w_in"].ap()[l], wi, 3072, gcol=g2)
            gq = S.sbuf(st, "gq", [128, 4], F32)
            for c, (nm, width) in enumerate([("q_norm_a", 64), ("k_norm_a", 64), ("q_norm_b", 32), ("k_norm_b", 32)]):
                for r in range(128 // width):
                    src = bass.AP(tensor=W[nm], offset=l * width, ap=[[1, width], [1, 1]])
                    S.dma("sp", gq[r * width:(r + 1) * width, c:c + 1], src, writes=[gq])
            S.dve(lambda e: e.tensor_scalar(out=gq[:, 0:1], in0=gq[:, 0:1], scalar1=HD ** -0.5, scalar2=None,
                                            op0=ALU.mult), reads=[gq], writes=[gq])
            S.dve(lambda e: e.tensor_scalar(out=gq[:, 2:3], in0=gq[:, 2:3], scalar1=32 ** -0.5, scalar2=None,
                                            op0=ALU.mult), reads=[gq], writes=[gq])
            f = self.alloc_ffn(st)
            xts = [S.sbuf(st, "xt%d" % i, [128, 4, 1024], F32) for i in range(2)]
            xtps = [[Buf("xt%d_%d" % (i, j)) for j in range(4)] for i in range(2)]
            mms = Rot([S.psum(st, "mm%d" % i, [128, 512], F32) for i in range(6)])
            tps = Rot([S.psum(st, "tp%d" % i, [128, 1024], BF16) for i in range(2)])
            qkt = S.sbuf(st, "qkt", [128, 8, 512], BF16)
            vxt = S.sbuf(st, "vxt", [128, 2, 16 * 128], BF16)
            qktp = [Buf("qkt%d" % i) for i in range(8)]
            vxtp = [Buf("vxt%d" % i) for i in range(2)]
            sqb = Rot([S.sbuf(st, "sqb%d" % i, [128, 512], BF16) for i in range(2)])
            srt = Rot([S.sbuf(st, "srt%d" % i, [128, 512], F32) for i in range(2)])
            S.pool(lambda e: e.memset(vxt[:], 1.0), writes=vxtp)
            chunks = ([(QA0 + i * 128, 0) for i in range(4)] + [(KA0 + i * 128, 1) for i in range(4)] +
                      [(QB0 + i * 128, 2) for i in range(2)] + [(KB0 + i * 128, 3) for i in range(2)] +
                      [(QC0 + i * 128, 4) for i in range(2)] + [(KC0 + i * 128, 5) for i in range(2)])
            for ti in range(self.NT):
                xt, xtp = xts[ti % 2], xtps[ti % 2]
                t0 = ti * 512
                S.dma("sp", xt[:], xin.ap()[t0:t0 + 512, :].rearrange("(j p) d -> p j d", p=128), writes=xtp)
                self.ffn_tile(f, xt, xtp, wg, wu, wd, mms, tps)
                S.dma("sp", self.xs.ap()[t0:t0 + 512, :].rearrange("(j p) d -> p j d", p=128), xt[:], reads=xtp)
                self.norm_transpose(f, xt, xtp, tps)
                hT = f["hT"]
                for c, (col0, kind) in enumerate(chunks):
                    pq = mms.next()
                    S.pe_group([lambda e, k=k, col0=col0, pq=pq: e.matmul(
                        pq[:], lhsT=wi[:, k, col0:col0 + 128], rhs=hT[:, k, :], start=(k == 0), stop=(k == 7))
                        for k in range(8)], reads=f["hTp"] + [wi], writes=[pq])
                    if kind <= 3:
                        sq = sqb.next()
                        S.act(lambda e, pq=pq, sq=sq: e.activation(out=sq[:], in_=pq[:], func=AF.Square),
                              reads=[pq], writes=[sq])
                        pm = mms.next()
                        bdc = C_BD64 if kind <= 1 else C_BD32
                        S.pe(lambda e, pm=pm, sq=sq, bdc=bdc: e.matmul(pm[:], lhsT=self.cb[:, bdc:bdc + 128], rhs=sq[:],
                                                                      start=True, stop=True),
                             reads=[sq, self.cb], writes=[pm])
                        sr = srt.next()
                        S.act(lambda e, pm=pm, sr=sr: e.activation(out=sr[:], in_=pm[:], func=AF.Sqrt, bias=EPS),
                              reads=[pm], writes=[sr])
                        rr = sr
                        S.dve(lambda e, sr=sr: e.reciprocal(out=sr[:], in_=sr[:]), reads=[sr], writes=[sr])
                        S.dve(lambda e, c=c, pq=pq, rr=rr, kind=kind: e.scalar_tensor_tensor(
                            out=qkt[:, c % 8, :], in0=pq[:], scalar=gq[:, kind:kind + 1], in1=rr[:], op0=ALU.mult,
                            op1=ALU.mult), reads=[pq, rr, gq], writes=[qktp[c % 8]])
                    elif kind == 4:
                        S.act(lambda e, c=c, pq=pq: e.activation(out=qkt[:, c % 8, :], in_=pq[:], func=AF.Copy,
                                                                 scale=HD ** -0.5), reads=[pq], writes=[qktp[c % 8]])
                    else:
                        S.act(lambda e, c=c, pq=pq: e.activation(out=qkt[:, c % 8, :], in_=pq[:], func=AF.Copy),
                              reads=[pq], writes=[qktp[c % 8]])
                    if c % 8 == 7:
                        r0 = (c // 8) * 1024
                        S.dma("sp", self.qk.ap()[r0:r0 + 1024, t0:t0 + 512].rearrange("(c p) t -> p c t", p=128),
                              qkt[:], reads=qktp)
                for j in range(4):
                    pv = mms.next()
                    S.pe_group([lambda e, k=k, j=j, pv=pv: e.matmul(
                        pv[:], lhsT=hT[:, k, j * 128:(j + 1) * 128], rhs=wi[:, k, VA0:VA0 + 512],
                        start=(k == 0), stop=(k == 7)) for k in range(8)], reads=f["hTp"] + [wi], writes=[pv])
                    S.act(lambda e, j=j, pv=pv: e.activation(
                        out=vxt[:, j % 2, 0:1024].rearrange("p (h e) -> p h e", e=128)[:, :, 0:64],
                        in_=pv[:].rearrange("p (h e) -> p h e", e=64), func=AF.Copy), reads=[pv], writes=[vxtp[j % 2]])
                    pv2 = mms.next()
                    fns = []
                    for g, vc0 in enumerate((VB0, VC0)):
                        for k in range(8):
                            fns.append(lambda e, k=k, j=j, g=g, vc0=vc0, pv2=pv2: e.matmul(
                                pv2[:, g * 256:(g + 1) * 256], lhsT=hT[:, k, j * 128:(j + 1) * 128],
                                rhs=wi[:, k, vc0:vc0 + 256], start=(k == 0), stop=(k == 7)))
                    S.pe_group(fns, reads=f["hTp"] + [wi], writes=[pv2])
                    S.dve(lambda e, j=j, pv2=pv2: e.tensor_copy(
                        out=vxt[:, j % 2, 1024:2048].rearrange("p (h e) -> p h e", e=128)[:, :, 0:64],
                        in_=pv2[:].rearrange("p (h e) -> p h e", e=64)), reads=[pv2], writes=[vxtp[j % 2]])
                    if j % 2 == 1:
                        tt0 = t0 + (j - 1) * 128
                        S.dma("sp", self.vx.ap()[tt0:tt0 + 256, :].rearrange("(j p) f -> p j f", p=128), vxt[:],
                              reads=vxtp)

    def phase_p3(self, l):
        S = self.S
        W = self.w
        with contextlib.ExitStack() as st:
            self._wl = 0
            stg = Rot([S.sbuf(st, "wstg%d" % i, [128, 1024], F32) for i in range(2)])
            wg = S.sbuf(st, "wg", [128, 8, 1024], BF16)
            wu = S.sbuf(st, "wu", [128, 8, 1024], BF16)
            wd = S.sbuf(st, "wd", [128, 8, 1024], BF16)
            wo = S.sbuf(st, "wo", [128, 8, 1024], BF16)
            g1 = S.sbuf(st, "g1", [128, 8], F32)
            self.load_gcol(g1, W["ffn2_norm"].ap()[l])
            self.load_weight(stg, W["w_out"].ap()[l], wo, 1024)
            self.load_weight(stg, W["ffn2_w_gate"].ap()[l], wg, 1024, gcol=g1)
            self.load_weight(stg, W["ffn2_w_up"].ap()[l], wu, 1024, gcol=g1)
            self.load_weight(stg, W["ffn2_w_down"].ap()[l], wd, 1024, cscale=0.5)
            f = self.alloc_ffn(st)
            xts = [S.sbuf(st, "xt%d" % i, [128, 4, 1024], F32) for i in range(2)]
            xtps = [[Buf("xt%d_%d" % (i, j)) for j in range(4)] for i in range(2)]
            mxs = [S.sbuf(st, "mx%d" % i, [128, 8, 512], BF16) for i in range(2)]
            mms = Rot([S.psum(st, "mm%d" % i, [128, 512], F32) for i in range(6)])
            tps = Rot([S.psum(st, "tp%d" % i, [128, 1024], BF16) for i in range(2)])
            for ti in range(self.NT):
                xt, xtp, mx = xts[ti % 2], xtps[ti % 2], mxs[ti % 2]
                t0 = ti * 512
                S.dma("sp", xt[:], self.xs.ap()[t0:t0 + 512, :].rearrange("(j p) d -> p j d", p=128), writes=xtp)
                S.dma("sp", mx[:], self.mixT.ap()[:, t0:t0 + 512].rearrange("(c p) t -> p c t", p=128), writes=[mx])
                for j in range(4):
                    for hf in range(2):
                        py = mms.next()
                        S.pe_group([lambda e, k=k, j=j, hf=hf, py=py: e.matmul(
                            py[:], lhsT=mx[:, k, j * 128:(j + 1) * 128], rhs=wo[:, k, hf * 512:(hf + 1) * 512],
                            start=(k == 0), stop=(k == 7)) for k in range(8)], reads=[mx, wo], writes=[py])
                        S.dve(lambda e, j=j, hf=hf, py=py: e.tensor_tensor(
                            out=xt[:, j, hf * 512:(hf + 1) * 512], in0=py[:], in1=xt[:, j, hf * 512:(hf + 1) * 512],
                            op=ALU.add), reads=[py, xtp[j]], writes=[xtp[j]])
                self.ffn_tile(f, xt, xtp, wg, wu, wd, mms, tps)
                dst = self.y if l == self.depth - 1 else self.xs
                S.dma("sp", dst.ap()[t0:t0 + 512, :].rearrange("(j p) d -> p j d", p=128), xt[:], reads=xtp)

    def load_head(self, qt_, kt_, vx_, qrow, krow, vhead):
        S = self.S
        S.dma("sp", qt_[0:64, :], self.qk.ap()[qrow:qrow + 64, :], writes=[qt_])
        S.dma("sp", kt_[0:64, :], self.qk.ap()[krow:krow + 64, :], writes=[kt_])
        if vx_ is not None:
            src = self.vx.ap()[:, vhead * 128:(vhead + 1) * 128].rearrange("(n p) e -> p n e", p=128)
            S.dma("sp", vx_[:], src, writes=[vx_])

    def phase_att(self, l):
        import os
        sel = os.environ.get("ATT_SEL", "cba")
        if "c" in sel:
            self.att_c(l)
            self.S.barrier()
        if "b" in sel:
            self.att_b(l)
            self.S.barrier()
        if "a" in sel:
            self.att_a(l)

    def att_c(self, l):
        S = self.S
        SL, NT, NB = self.SL, self.NT, self.NB
        cb = self.cb
        with contextlib.ExitStack() as st:
            qts = [S.sbuf(st, "cq%d" % i, [128, SL], BF16) for i in range(2)]
            kts = [S.sbuf(st, "ck%d" % i, [128, SL], BF16) for i in range(2)]
            for t_ in qts + kts:
                S.pool(lambda e, t_=t_: e.memset(t_[64:128, :], 0.0), writes=[t_])
            vxs = [S.sbuf(st, "cv%d" % i, [128, NB, 128], BF16) for i in range(2)]
            ets = [S.sbuf(st, "ce%d" % i, [128, 512], F32) for i in range(2)]
            lbs = [S.sbuf(st, "cl%d" % i, [128, 512], BF16) for i in range(3)]
            abs_ = [S.sbuf(st, "ca%d" % i, [128, 512], BF16) for i in range(2)]
            lsf = S.sbuf(st, "clsf", [128, 512], F32)
            lsb = [S.sbuf(st, "clsb%d" % i, [128, 512], BF16) for i in range(4)]
            osb = Rot([S.sbuf(st, "cos%d" % i, [64, 512], BF16) for i in range(2)])
            z1s = [S.psum(st, "cz1_%d" % i, [128, 512], F32) for i in range(2)]
            z2s = [S.psum(st, "cz2_%d" % i, [128, 512], F32) for i in range(2)]
            ots = [S.psum(st, "cot%d" % i, [128, 512], F32) for i in range(2)]
            ident = cb[:, C_IDENT:C_IDENT + 128]
            for h in range(4):
                if h == 0:
                    self.load_head(qts[0], kts[0], vxs[0], 1536 + 0 * 64, 1792 + 0 * 64, 12)
                if h + 1 < 4:
                    self.load_head(qts[(h + 1) % 2], kts[(h + 1) % 2], vxs[(h + 1) % 2],
                                   1536 + (h + 1) * 64, 1792 + (h + 1) * 64, 12 + h + 1)
                qT, kT, vx = qts[h % 2], kts[h % 2], vxs[h % 2]
                steps = []
                for qt in range(NT):
                    kbs = list(range(4 * qt + 3, -1, -1))
                    for idx, kb in enumerate(kbs):
                        steps.append((qt, idx, kb, len(kbs)))
                n = len(steps)

                def zmm(zb, qt, kb, extra, first_stop):
                    fns = [lambda e: e.matmul(zb[:], lhsT=kT[0:64, kb * 128:(kb + 1) * 128],
                                              rhs=qT[0:64, qt * 512:(qt + 1) * 512], start=True, stop=first_stop)]
                    return fns

                def stageA(i):
                    qt, idx, kb, nk = steps[i]
                    zb = z1s[i % 2]
                    diag = kb >= 4 * qt
                    fns = [lambda e: e.matmul(zb[:], lhsT=kT[:, kb * 128:(kb + 1) * 128],
                                              rhs=qT[:, qt * 512:(qt + 1) * 512], start=True, stop=not diag)]
                    if diag:
                        c = kb - 4 * qt
                        fns.append(lambda e: e.matmul(zb[:], lhsT=ident,
                                                      rhs=cb[:, C_MASKC + c * 512:C_MASKC + (c + 1) * 512],
                                                      start=False, stop=True))
                    S.pe_group(fns, reads=[qT, kT, cb], writes=[zb])

                def stageB(i):
                    qt, idx, kb, nk = steps[i]
                    zb, et, lb = z1s[i % 2], ets[i % 2], lbs[i % 3]
                    S.act(lambda e: e.activation(out=et[:], in_=zb[:], func=AF.Exp), reads=[zb], writes=[et])
                    S.act(lambda e: e.activation(out=lb[:], in_=et[:], func=AF.Ln, bias=1.0), reads=[et], writes=[lb])
                    if idx < nk - 1:
                        if idx == 0:
                            S.pool(lambda e: e.tensor_copy(out=lsf[:], in_=lb[:]), reads=[lb], writes=[lsf])
                        else:
                            S.pool(lambda e: e.tensor_tensor(out=lsf[:], in0=lsf[:], in1=lb[:], op=ALU.add),
                                   reads=[lb, lsf], writes=[lsf])
                        nb_ = lsb[(i + 1) % 4]
                        S.dve(lambda e: e.tensor_copy(out=nb_[:], in_=lsf[:]), reads=[lsf], writes=[nb_])

                def stageC(i):
                    qt, idx, kb, nk = steps[i]
                    zb, lb = z2s[i % 2], lbs[i % 3]
                    diag = kb >= 4 * qt
                    fns = [lambda e: e.matmul(zb[:], lhsT=kT[:, kb * 128:(kb + 1) * 128],
                                              rhs=qT[:, qt * 512:(qt + 1) * 512], start=True, stop=False)]
                    if diag:
                        c = kb - 4 * qt
                        fns.append(lambda e: e.matmul(zb[:], lhsT=ident,
                                                      rhs=cb[:, C_MASKC + c * 512:C_MASKC + (c + 1) * 512],
                                                      start=False, stop=False))
                    fns.append(lambda e: e.matmul(zb[:], lhsT=cb[:, C_TRI:C_TRI + 128], rhs=lb[:], start=False,
                                                  stop=(idx == 0)))
                    rd = [qT, kT, cb, lb]
                    if idx > 0:
                        cur = lsb[i % 4]
                        fns.append(lambda e: e.matmul(zb[:], lhsT=cb[:, C_ONES:C_ONES + 128], rhs=cur[:],
                                                      start=False, stop=True))
                        rd.append(cur)
                    S.pe_group(fns, reads=rd, writes=[zb])

                def stageD(i):
                    zb, ab = z2s[i % 2], abs_[i % 2]
                    S.act(lambda e: e.activation(out=ab[:], in_=zb[:], func=AF.Exp), reads=[zb], writes=[ab])

                def stageE(i):
                    qt, idx, kb, nk = steps[i]
                    ab, ot = abs_[i % 2], ots[qt % 2]
                    S.pe(lambda e: e.matmul(ot[0:64, :], lhsT=vx[:, kb, 0:64], rhs=ab[:], start=(idx == 0),
                                            stop=(idx == nk - 1)), reads=[ab, vx], writes=[ot])
                    if idx == nk - 1:
                        ob = osb.next()
                        S.dve(lambda e: e.tensor_copy(out=ob[:], in_=ot[0:64, :]), reads=[ot], writes=[ob])
                        row = 768 + h * 64
                        S.dma("sp", self.mixT.ap()[row:row + 64, qt * 512:(qt + 1) * 512], ob[:], reads=[ob])

                for t in range(n + 3):
                    if t < n:
                        stageA(t)
                        stageB(t)
                    if 0 <= t - 2 < n:
                        stageC(t - 2)
                        stageD(t - 2)
                    if 0 <= t - 3 < n:
                        stageE(t - 3)

    def att_b(self, l):
        S = self.S
        SL, NT, NB = self.SL, self.NT, self.NB
        W = self.w
        cb = self.cb
        lam_init = 0.8 - 0.6 * math.exp(-0.3 * l)
        with contextlib.ExitStack() as st:
            qts = [S.sbuf(st, "bq%d" % i, [128, SL], BF16) for i in range(2)]
            kts = [[S.sbuf(st, "bk%d_%d" % (i, m), [128, SL], BF16) for m in range(2)] for i in range(2)]
            for t_ in qts:
                S.pool(lambda e, t_=t_: e.memset(t_[64:128, :], 0.0), writes=[t_])
            for i in range(2):
                S.pool(lambda e, i=i: e.memset(kts[i][0][32:64, :], 0.0), writes=[kts[i][0]])
                S.pool(lambda e, i=i: e.memset(kts[i][0][64:128, :], 0.0), writes=[kts[i][0]])
                S.pool(lambda e, i=i: e.memset(kts[i][1][0:32, :], 0.0), writes=[kts[i][1]])
                S.pool(lambda e, i=i: e.memset(kts[i][1][64:128, :], 0.0), writes=[kts[i][1]])
            vxs = [S.sbuf(st, "bv%d" % i, [128, NB, 128], BF16) for i in range(2)]
            tbf = S.sbuf(st, "btbf", [128, WT_B], F32)
            tbs = [S.sbuf(st, "btb%d" % i, [128, WT_B], BF16) for i in range(2)]
            pbs = [S.sbuf(st, "bpb%d" % i, [128, 1024], BF16) for i in range(3)]
            ident = cb[:, C_IDENT:C_IDENT + 128]
            s1s = [S.psum(st, "bs1_%d" % i, [128, 1024], F32) for i in range(2)]
            obs = [[S.psum(st, "bo%d_%d" % (i, m), [128, 512], F32) for m in range(2)] for i in range(2)]
            lam4 = S.sbuf(st, "blam4", [128, 4, 32], F32)
            for i, nm in enumerate(["lambda_q1", "lambda_k1", "lambda_q2", "lambda_k2"]):
                src = bass.AP(tensor=W[nm], offset=l * 32, ap=[[0, 128], [1, 32]])
                S.dma("sp", lam4[:, i, :], src, writes=[lam4])
            lp = S.sbuf(st, "blp", [128, 2, 32], F32)
            S.dve(lambda e: e.tensor_tensor(out=lp[:, 0, :], in0=lam4[:, 0, :], in1=lam4[:, 1, :], op=ALU.mult),
                  reads=[lam4], writes=[lp])
            S.dve(lambda e: e.tensor_tensor(out=lp[:, 1, :], in0=lam4[:, 2, :], in1=lam4[:, 3, :], op=ALU.mult),
                  reads=[lam4], writes=[lp])
            ls = S.sbuf(st, "bls", [128, 2], F32)
            S.dve(lambda e: e.tensor_reduce(out=ls[:], in_=lp[:], axis=mybir.AxisListType.X, op=ALU.add),
                  reads=[lp], writes=[ls])
            le = S.sbuf(st, "ble", [128, 2], F32)
            S.act(lambda e: e.activation(out=le[:], in_=ls[:], func=AF.Exp), reads=[ls], writes=[le])
            neglam = S.sbuf(st, "bneglam", [128, 1], F32)
            S.dve(lambda e: e.tensor_tensor(out=neglam[:], in0=le[:, 1:2], in1=le[:, 0:1], op=ALU.subtract),
                  reads=[le], writes=[neglam])
            S.dve(lambda e: e.tensor_scalar(out=neglam[:], in0=neglam[:], scalar1=-lam_init, scalar2=None,
                                            op0=ALU.add), reads=[neglam], writes=[neglam])
            gsub = S.sbuf(st, "bgsub", [128, 1], F32)
            src = bass.AP(tensor=W["diff_subln"], offset=l * 64, ap=[[1, 64], [1, 1]])
            S.dma("sp", gsub[0:64, :], src, writes=[gsub])
            S.dve(lambda e: e.tensor_scalar(out=gsub[0:64, :], in0=gsub[0:64, :], scalar1=1.0 - lam_init,
                                            scalar2=None, op0=ALU.mult), reads=[gsub], writes=[gsub])
            rts = [S.sbuf(st, "brt%d" % i, [64, 512], F32) for i in range(2)]
            nts = [S.sbuf(st, "bnt%d" % i, [64, 512], F32) for i in range(2)]
            dts = S.sbuf(st, "bdt", [64, 512], F32)
            sqd = S.sbuf(st, "bsqd", [64, 512], BF16)
            lnm = S.sbuf(st, "blnm", [64, 512], F32)
            rsm = S.sbuf(st, "brsm", [64, 512], F32)
            osb = Rot([S.sbuf(st, "bos%d" % i, [64, 512], BF16) for i in range(2)])

            def load_b(h, slot):
                qrow, krow = 1024 + h * 64, 1280 + h * 64
                S.dma("sp", qts[slot][0:64, :], self.qk.ap()[qrow:qrow + 64, :], writes=[qts[slot]])
                S.dma("sp", kts[slot][0][0:32, :], self.qk.ap()[krow:krow + 32, :], writes=[kts[slot][0]])
                S.dma("sp", kts[slot][1][32:64, :], self.qk.ap()[krow + 32:krow + 64, :], writes=[kts[slot][1]])
                src = self.vx.ap()[:, (8 + h) * 128:(9 + h) * 128].rearrange("(n p) e -> p n e", p=128)
                S.dma("sp", vxs[slot][:], src, writes=[vxs[slot]])
                S.dma("sp", tbf[:], self.tb_d.ap()[h], writes=[tbf])
                S.dve(lambda e: e.tensor_copy(out=tbs[slot][:], in_=tbf[:]), reads=[tbf], writes=[tbs[slot]])

            for h in range(4):
                if h == 0:
                    load_b(0, 0)
                if h + 1 < 4:
                    load_b(h + 1, (h + 1) % 2)
                qT, kTm, vx, tb = qts[h % 2], kts[h % 2], vxs[h % 2], tbs[h % 2]
                steps = []
                for qt in range(NT):
                    nk = 4 * qt + 4
                    for kb in range(nk):
                        steps.append((qt, kb, nk))
                n = len(steps)

                def stageA(i):
                    qt, kb, nk = steps[i]
                    sb_ = s1s[i % 2]
                    Dd = qt * 512 - kb * 128
                    near = Dd < FAR_D
                    fns = []
                    for m in range(2):
                        kT = kTm[m]
                        fns.append(lambda e, m=m, kT=kT: e.matmul(
                            sb_[:, m * 512:(m + 1) * 512], lhsT=kT[:, kb * 128:(kb + 1) * 128],
                            rhs=qT[:, qt * 512:(qt + 1) * 512], start=True, stop=not near))
                        if near:
                            c0 = Dd + 384
                            fns.append(lambda e, m=m, c0=c0: e.matmul(
                                sb_[:, m * 512:(m + 1) * 512], lhsT=ident, rhs=tb[:, c0:c0 + 512], start=False,
                                stop=True))
                    S.pe_group(fns, reads=[qT, kTm[0], kTm[1], tb, cb], writes=[sb_])

                def stageB(i):
                    s_, pb = s1s[i % 2], pbs[i % 3]
                    S.act(lambda e: e.activation(out=pb[:], in_=s_[:], func=AF.Exp), reads=[s_], writes=[pb])

                def stageC(i):
                    qt, kb, nk = steps[i]
                    pb = pbs[i % 3]
                    o0, o1 = obs[qt % 2]
                    pmb = o0
                    S.pe_group([lambda e, m=m, ob=ob: e.matmul(ob[:], lhsT=vx[:, kb, :], rhs=pb[:, m * 512:(m + 1) * 512],
                                                               start=(kb == 0), stop=(kb == nk - 1))
                                for m, ob in ((0, o0), (1, o1))], reads=[pb, vx], writes=[o0, o1])
                    if kb == nk - 1:
                        o0, o1 = obs[qt % 2]
                        for mm_, o_ in ((0, o0), (1, o1)):
                            rt, nt = rts[mm_], nts[mm_]
                            S.dve(lambda e, o_=o_, rt=rt: e.reciprocal(out=rt[:], in_=o_[64:128, :]),
                                  reads=[o_], writes=[rt])
                            S.dve(lambda e, o_=o_, rt=rt, nt=nt: e.tensor_tensor(out=nt[:], in0=o_[0:64, :], in1=rt[:],
                                                                                 op=ALU.mult),
                                  reads=[o_, rt], writes=[nt])
                        S.dve(lambda e: e.scalar_tensor_tensor(out=dts[:], in0=nts[1][:], scalar=neglam[0:64, :],
                                                               in1=nts[0][:], op0=ALU.mult, op1=ALU.add),
                              reads=[nts[0], nts[1], neglam], writes=[dts])
                        S.act(lambda e: e.activation(out=sqd[:], in_=dts[:], func=AF.Square), reads=[dts], writes=[sqd])
                        S.pe(lambda e: e.matmul(pmb[0:64, :], lhsT=cb[0:64, C_BD64:C_BD64 + 64], rhs=sqd[:],
                                                start=True, stop=True), reads=[sqd, cb], writes=[pmb])
                        S.act(lambda e: e.activation(out=lnm[:], in_=pmb[0:64, :], func=AF.Ln, bias=EPS),
                              reads=[pmb], writes=[lnm])
                        S.act(lambda e: e.activation(out=rsm[:], in_=lnm[:], func=AF.Exp, scale=-0.5),
                              reads=[lnm], writes=[rsm])
                        ob_ = osb.next()
                        S.dve(lambda e: e.scalar_tensor_tensor(out=ob_[:], in0=dts[:], scalar=gsub[0:64, :],
                                                               in1=rsm[:], op0=ALU.mult, op1=ALU.mult),
                              reads=[dts, rsm, gsub], writes=[ob_])
                        row = 512 + h * 64
                        S.dma("sp", self.mixT.ap()[row:row + 64, qt * 512:(qt + 1) * 512], ob_[:], reads=[ob_])

                for t in range(n + 2):
                    if t < n:
                        stageA(t)
                    if 0 <= t - 1 < n:
                        stageB(t - 1)
                    if 0 <= t - 2 < n:
                        stageC(t - 2)

    def att_a(self, l):
        S = self.S
        SL, NT, NB = self.SL, self.NT, self.NB
        with contextlib.ExitStack() as st:
            qts = [S.sbuf(st, "aq%d" % i, [128, SL], BF16) for i in range(2)]
            kts = [S.sbuf(st, "ak%d" % i, [128, SL], BF16) for i in range(2)]
            for t_ in qts + kts:
                S.pool(lambda e, t_=t_: e.memset(t_[64:128, :], 0.0), writes=[t_])
            vxs = [S.sbuf(st, "av%d" % i, [128, NB, 128], BF16) for i in range(2)]
            taf = S.sbuf(st, "ataf", [128, 3 * 256], F32)
            tas = [S.sbuf(st, "ata%d" % i, [128, 3 * 256], BF16) for i in range(2)]
            ident = self.cb[:, C_IDENT:C_IDENT + 128]
            acc = S.sbuf(st, "aacc", [128, SL], F32)
            pbs = [S.sbuf(st, "apb%d" % i, [128, 256], BF16) for i in range(3)]
            s1s = [S.psum(st, "as1_%d" % i, [128, 512], F32) for i in range(3)]
            regs = [S.psum(st, "areg%d" % i, [128, 512], F32) for i in range(4)]
            regb = [Buf("aregb%d" % i) for i in range(16)]
            rct = Rot([S.sbuf(st, "arc%d" % i, [64, 512], F32) for i in range(2)])
            osb = Rot([S.sbuf(st, "aos%d" % i, [64, 512], BF16) for i in range(2)])
            vcount = [0]

            def load_v(h, bi):
                w, dil = BRANCHES[bi]
                nb = SL // dil // 128
                vx_ = vxs[vcount[0] % 2]
                vcount[0] += 1
                for r in range(dil):
                    src = bass.AP(tensor=self.vx, offset=r * 2048 + h * 128,
                                  ap=[[dil * 2048, 128], [128 * dil * 2048, nb], [1, 128]])
                    S.dma("sp", vx_[:, r * nb:(r + 1) * nb, :], src, writes=[vx_])
                return vx_

            def load_qk(h, slot):
                self.load_head(qts[slot], kts[slot], None, h * 64, 512 + h * 64, None)
                S.dma("sp", taf[:], self.ta_d.ap()[h], writes=[taf])
                S.dve(lambda e: e.tensor_copy(out=tas[slot][:], in_=taf[:]), reads=[taf], writes=[tas[slot]])

            qid = [0]
            for h in range(8):
                if h == 0:
                    load_qk(0, 0)
                if h + 1 < 8:
                    load_qk(h + 1, (h + 1) % 2)
                qT, kT, ta = qts[h % 2], kts[h % 2], tas[h % 2]
                for bi, (w, dil) in enumerate(BRANCHES):
                    nb = SL // dil // 128
                    vx = load_v(h, bi)
                    S._wait("dve", ("dve", S.cnt["dve"]))
                    steps = [(r, n_) for r in range(dil) for n_ in range(nb)]
                    n = len(steps)
                    info = {}

                    def tok(r, n_, cnt):
                        start = (n_ * 128) * dil + r
                        return slice(start, start + (cnt - 1) * dil + 1, dil)

                    def stageA(i):
                        r, n_ = steps[i]
                        nq = 256 if n_ + 1 < nb else 128
                        sb_ = s1s[i % 3]
                        S.pe_group([
                            lambda e: e.matmul(sb_[:, 0:nq], lhsT=kT[:, tok(r, n_, 128)], rhs=qT[:, tok(r, n_, nq)],
                                               start=True, stop=False),
                            lambda e: e.matmul(sb_[:, 0:nq], lhsT=ident, rhs=ta[:, bi * 256:bi * 256 + nq],
                                               start=False, stop=True)],
                            reads=[qT, kT, ta, self.cb], writes=[sb_])

                    def stageB(i):
                        r, n_ = steps[i]
                        nq = 256 if n_ + 1 < nb else 128
                        s_, pb = s1s[i % 3], pbs[i % 3]
                        S.act(lambda e: e.activation(out=pb[:, 0:nq], in_=s_[:, 0:nq], func=AF.Exp),
                              reads=[s_], writes=[pb])

                    def stageC(i):
                        r, n_ = steps[i]
                        pb = pbs[i % 3]
                        blk = r * nb + n_
                        if n_ == 0:
                            info[(r, 0)] = qid[0]
                            qid[0] += 1
                        q_cur = info[(r, n_)]
                        rg = regs[q_cur % 4]
                        cs = 0
                        rb_ = regb[q_cur % 4]
                        S.pe(lambda e: e.matmul(rg[:, cs:cs + 128], lhsT=vx[:, blk, :], rhs=pb[:, 0:128],
                                                start=(n_ == 0), stop=True), reads=[pb, vx], writes=[rb_])
                        tsl = tok(r, n_, 128)
                        if bi == 0:
                            S.dve(lambda e: e.tensor_copy(out=acc[:, tsl], in_=rg[:, cs:cs + 128]),
                                  reads=[rb_], writes=[])
                        else:
                            S.dve(lambda e: e.tensor_tensor(out=acc[:, tsl], in0=rg[:, cs:cs + 128], in1=acc[:, tsl],
                                                            op=ALU.add), reads=[rb_], writes=[])
                        if n_ + 1 < nb:
                            info[(r, n_ + 1)] = qid[0]
                            qid[0] += 1
                            q_n = info[(r, n_ + 1)]
                            rg2 = regs[q_n % 4]
                            cs2 = 0
                            S.pe(lambda e: e.matmul(rg2[:, cs2:cs2 + 128], lhsT=vx[:, blk, :], rhs=pb[:, 128:256],
                                                    start=True, stop=False), reads=[pb, vx], writes=[regb[q_n % 4]])

                    for t in range(n + 2):
                        if t < n:
                            stageA(t)
                        if 0 <= t - 1 < n:
                            stageB(t - 1)
                        if 0 <= t - 2 < n:
                            stageC(t - 2)
                S._wait("dve", ("dve", S.cnt["dve"]))
                accb = Buf("accb")
                for ci in range(SL // 512):
                    rc = rct.next()
                    ob = osb.next()
                    S.dve(lambda e: e.reciprocal(out=rc[:], in_=acc[64:128, ci * 512:(ci + 1) * 512]),
                          reads=[accb], writes=[rc])
                    S.dve(lambda e: e.tensor_tensor(out=ob[:], in0=acc[0:64, ci * 512:(ci + 1) * 512], in1=rc[:],
                                                    op=ALU.mult), reads=[accb, rc], writes=[ob])
                    S.dma("sp", self.mixT.ap()[h * 64:(h + 1) * 64, ci * 512:(ci + 1) * 512], ob[:], reads=[ob])


_CACHE = {}


def get_prog(S_len, depth, dbg=False, phases=None):
    key = (S_len, depth, dbg, tuple(phases) if phases else None)
    if key not in _CACHE:
        p = Prog(S_len, depth, dbg, phases)
        p.build()
        _CACHE[key] = p
    return _CACHE[key]


def kernel(**inputs):
    x = np.asarray(inputs["x"], dtype=np.float32)
    B, S_len, _ = x.shape
    depth = int(np.asarray(inputs["w_in"]).shape[0])
    prog = get_prog(S_len, depth)
    consts, onehot = make_consts()
    shared = {k: np.ascontiguousarray(np.asarray(v, dtype=np.float32)) for k, v in inputs.items() if k != "x"}
    shared["consts"] = consts
    shared["onehot"] = onehot
    in_maps = []
    for b in range(B):
        m = dict(shared)
        m["x"] = np.ascontiguousarray(x[b])
        in_maps.append(m)
    res = run_bass_kernel_spmd(prog.nc, in_maps, core_ids=list(range(B)))
    return np.stack([np.asarray(r["y"]) for r in res.results], axis=0).astype(np.float32)
```
